# Optimizing a Trainium2 kernel written in Bass

```python
import math
import jax, jax.numpy as jnp
from jax import lax
import numpy as np

D_MODEL = 1024
BATCH = 4
SEQ = 4096
DEPTH = 2

D_MIX = D_MODEL
N_DIR = 2
RG_WIDTH = 384
RG_HEADS = 6
RG_HEAD_DIM = RG_WIDTH // RG_HEADS
RG_CONV = 4
RG_C = 8.0
S5_WIDTH = 384
S5_GROUP = 16
S5_GROUPS = S5_WIDTH // S5_GROUP
S5_STATE = 64
HY_WIDTH = D_MIX - RG_WIDTH - S5_WIDTH
HY_CONV = 3
HY_BANDS = 16
HY_FEAT = 1 + 2 * HY_BANDS
HY_FILT_HIDDEN = 64
HY_FAST_DECAY = 0.3
HY_SLOW_DECAY = 1.5
HY_DECAY_TARGET = 1e-2
D_FF = 2816
FFN_CONV = 3
RMS_EPS = 1e-6
IN_COLS = 2 * RG_WIDTH + S5_WIDTH + 3 * HY_WIDTH

kernel_name = "hymba_style_rglru_s5_hyena_encoder"


def rmsnorm(x, g):
    xf = x.astype(jnp.float32)
    inv = lax.rsqrt(jnp.mean(xf * xf, axis=-1, keepdims=True) + RMS_EPS)
    return xf * inv * g.astype(jnp.float32)


def rms_nogain(x):
    xf = x.astype(jnp.float32)
    return xf * lax.rsqrt(jnp.mean(xf * xf, axis=-1, keepdims=True) + RMS_EPS)


def dwconv_centred(u, w, b):
    K = w.shape[0]
    L = u.shape[1]
    left = (K - 1) // 2
    up = jnp.pad(u, ((0, 0), (left, K - 1 - left), (0, 0)))
    out = b
    for k in range(K):
        out = out + up[:, k:k + L] * w[k]
    return out


def _lin_combine(e1, e2):
    a1, b1 = e1
    a2, b2 = e2
    return a1 * a2, a2 * b1 + b2


def _clin_combine(e1, e2):
    ar1, ai1, br1, bi1 = e1
    ar2, ai2, br2, bi2 = e2
    return (ar2 * ar1 - ai2 * ai1,
            ar2 * ai1 + ai2 * ar1,
            ar2 * br1 - ai2 * bi1 + br2,
            ar2 * bi1 + ai2 * br1 + bi2)


def rglru_mixer(xb, gb, conv_w, conv_b, wa, ba, wx, bx, lam):
    Bsz, L, _ = xb.shape
    u = dwconv_centred(xb, conv_w, conv_b)
    uh = u.reshape(Bsz, L, RG_HEADS, RG_HEAD_DIM)
    h_sum = jnp.zeros_like(u)
    for d in range(N_DIR):
        r = jax.nn.sigmoid(jnp.einsum('blhi,hij->blhj', uh, wa[d]).reshape(Bsz, L, RG_WIDTH) + ba[d])
        i = jax.nn.sigmoid(jnp.einsum('blhi,hij->blhj', uh, wx[d]).reshape(Bsz, L, RG_WIDTH) + bx[d])
        log_a = -RG_C * r * jax.nn.softplus(-lam[d].astype(jnp.float32))
        a = jnp.exp(log_a)
        mult = jnp.sqrt(-jnp.expm1(2.0 * log_a))
        _, h = lax.associative_scan(_lin_combine, (a, mult * (i * u)), reverse=(d == 1), axis=1)
        h_sum = h_sum + h
    return h_sum * jax.nn.gelu(gb)


def s5_mixer(u, a_re, a_im, log_dt, b_re, b_im, c_re, c_im, d_skip, glu_w, glu_b):
    Bsz, L, _ = u.shape
    ug = u.reshape(Bsz, L, S5_GROUPS, S5_GROUP)
    y = ug * d_skip.reshape(S5_GROUPS, S5_GROUP)
    for d in range(N_DIR):
        lr = a_re[d].astype(jnp.float32)
        li = a_im[d].astype(jnp.float32)
        dt = jnp.exp(log_dt[d].astype(jnp.float32))[:, None]
        mag = jnp.exp(lr * dt)
        abar_r = mag * jnp.cos(li * dt)
        abar_i = mag * jnp.sin(li * dt)
        den = lr * lr + li * li
        nr = abar_r - 1.0
        ni = abar_i
        coef_r = (nr * lr + ni * li) / den
        coef_i = (ni * lr - nr * li) / den
        bbar_r = coef_r[..., None] * b_re[d] - coef_i[..., None] * b_im[d]
        bbar_i = coef_r[..., None] * b_im[d] + coef_i[..., None] * b_re[d]
        bu_r = jnp.einsum('blgc,gpc->blgp', ug, bbar_r)
        bu_i = jnp.einsum('blgc,gpc->blgp', ug, bbar_i)
        ar = jnp.broadcast_to(abar_r, bu_r.shape)
        ai = jnp.broadcast_to(abar_i, bu_r.shape)
        _, _, hr, hi = lax.associative_scan(_clin_combine, (ar, ai, bu_r, bu_i), reverse=(d == 1), axis=1)
        y = y + jnp.einsum('blgp,gcp->blgc', hr, c_re[d]) - jnp.einsum('blgp,gcp->blgc', hi, c_im[d])
    y = jax.nn.gelu(y.reshape(Bsz, L, S5_WIDTH))
    return y * jax.nn.sigmoid(y @ glu_w + glu_b)


def hyena_filters(L, w1, b1, freq1, w2, b2, freq2, w3):
    pos = jnp.arange(L, dtype=jnp.float32)
    t = pos / max(L - 1, 1)
    w = (2.0 * math.pi / L) * pos
    bands = jnp.linspace(1e-4, HY_BANDS - 1, HY_BANDS, dtype=jnp.float32)
    ang = w[:, None] * bands
    feats = jnp.concatenate([t[:, None], jnp.cos(ang), -jnp.sin(ang)], axis=-1)
    hid = jnp.sin(freq1 * (feats @ w1 + b1))
    hid = jnp.sin(freq2 * (hid @ w2 + b2))
    k = hid @ w3
    max_decay = math.log(HY_DECAY_TARGET) / HY_FAST_DECAY
    min_decay = math.log(HY_DECAY_TARGET) / HY_SLOW_DECAY
    deltas = jnp.abs(jnp.linspace(min_decay, max_decay, HY_WIDTH, dtype=jnp.float32))
    k = k * jnp.exp(-t[:, None] * jnp.tile(deltas, N_DIR))
    return k[:, :HY_WIDTH], k[:, HY_WIDTH:]


def bidir_fft_conv(z, k_fwd, k_bwd):
    L = z.shape[1]
    zero = jnp.zeros((1, HY_WIDTH), jnp.float32)
    kern = jnp.concatenate([k_fwd[:1] + k_bwd[:1], k_fwd[1:], zero, k_bwd[:0:-1]], axis=0)
    zf = jnp.fft.rfft(z.astype(jnp.float32), n=2 * L, axis=1)
    kf = jnp.fft.rfft(kern, axis=0)
    return jnp.fft.irfft(zf * kf, n=2 * L, axis=1)[:, :L]


def hyena_mixer(pc, conv_w, conv_b, w1, b1, freq1, w2, b2, freq2, w3, bias):
    L = pc.shape[1]
    q = dwconv_centred(pc, conv_w, conv_b)
    x0, x1, v = jnp.split(q, 3, axis=-1)
    k_fwd, k_bwd = hyena_filters(L, w1, b1, freq1, w2, b2, freq2, w3)
    z = v * x1
    z = bidir_fft_conv(z, k_fwd, k_bwd) + z * bias
    return z * x0


def conv_ffn(h, w_up, conv_w, conv_b, w_down):
    u = dwconv_centred(h @ w_up, conv_w, conv_b)
    a, v = jnp.split(u, 2, axis=-1)
    return (jax.nn.gelu(a) * v) @ w_down


def setup_inputs(seed: int = 0) -> dict:
    key = jax.random.key(seed)
    ks = jax.random.split(key, 40)
    f32 = jnp.float32

    def nrm(k, shape, scale):
        return jax.random.normal(k, shape, f32) * scale

    def gain(k, shape):
        return 1.0 + 0.02 * jax.random.normal(k, shape, f32)

    a_target = jax.random.uniform(ks[9], (DEPTH, N_DIR, RG_WIDTH), f32, 0.9, 0.999)
    s = a_target ** (1.0 / RG_C)
    rg_lambda = jnp.log(s) - jnp.log1p(-s)
    n_idx = jnp.arange(S5_STATE, dtype=f32)
    return {
        "x": nrm(ks[0], (BATCH, SEQ, D_MODEL), 1.0),
        "norm1_g": gain(ks[1], (DEPTH, D_MODEL)),
        "w_in": nrm(ks[2], (DEPTH, D_MODEL, IN_COLS), D_MODEL ** -0.5),
        "rg_conv_w": nrm(ks[3], (DEPTH, RG_CONV, RG_WIDTH), RG_CONV ** -0.5),
        "rg_conv_b": nrm(ks[4], (DEPTH, RG_WIDTH), 0.01),
        "rg_wa": nrm(ks[5], (DEPTH, N_DIR, RG_HEADS, RG_HEAD_DIM, RG_HEAD_DIM), RG_HEAD_DIM ** -0.5),
        "rg_ba": nrm(ks[6], (DEPTH, N_DIR, RG_WIDTH), 0.01),
        "rg_wx": nrm(ks[7], (DEPTH, N_DIR, RG_HEADS, RG_HEAD_DIM, RG_HEAD_DIM), RG_HEAD_DIM ** -0.5),
        "rg_bx": nrm(ks[8], (DEPTH, N_DIR, RG_WIDTH), 0.01),
        "rg_lambda": rg_lambda,
        "s5_a_re": -0.5 + nrm(ks[10], (DEPTH, N_DIR, S5_GROUPS, S5_STATE), 0.01),
        "s5_a_im": math.pi * n_idx + nrm(ks[11], (DEPTH, N_DIR, S5_GROUPS, S5_STATE), 0.01),
        "s5_log_dt": jax.random.uniform(ks[12], (DEPTH, N_DIR, S5_GROUPS), f32, math.log(1e-3), math.log(1e-1)),
        "s5_b_re": nrm(ks[13], (DEPTH, N_DIR, S5_GROUPS, S5_STATE, S5_GROUP), (2 * S5_GROUP) ** -0.5),
        "s5_b_im": nrm(ks[14], (DEPTH, N_DIR, S5_GROUPS, S5_STATE, S5_GROUP), (2 * S5_GROUP) ** -0.5),
        "s5_c_re": nrm(ks[15], (DEPTH, N_DIR, S5_GROUPS, S5_GROUP, S5_STATE), (2 * S5_STATE) ** -0.5),
        "s5_c_im": nrm(ks[16], (DEPTH, N_DIR, S5_GROUPS, S5_GROUP, S5_STATE), (2 * S5_STATE) ** -0.5),
        "s5_d": nrm(ks[17], (DEPTH, S5_WIDTH), 1.0),
        "s5_glu_w": nrm(ks[18], (DEPTH, S5_WIDTH, S5_WIDTH), S5_WIDTH ** -0.5),
        "s5_glu_b": nrm(ks[19], (DEPTH, S5_WIDTH), 0.01),
        "hy_conv_w": nrm(ks[20], (DEPTH, HY_CONV, 3 * HY_WIDTH), HY_CONV ** -0.5),
        "hy_conv_b": nrm(ks[21], (DEPTH, 3 * HY_WIDTH), 0.01),
        "hy_filt_w1": nrm(ks[22], (DEPTH, HY_FEAT, HY_FILT_HIDDEN), HY_FEAT ** -0.5),
        "hy_filt_b1": nrm(ks[23], (DEPTH, HY_FILT_HIDDEN), 0.1),
        "hy_filt_freq1": gain(ks[24], (DEPTH, HY_FILT_HIDDEN)),
        "hy_filt_w2": nrm(ks[25], (DEPTH, HY_FILT_HIDDEN, HY_FILT_HIDDEN), HY_FILT_HIDDEN ** -0.5),
        "hy_filt_b2": nrm(ks[26], (DEPTH, HY_FILT_HIDDEN), 0.1),
        "hy_filt_freq2": gain(ks[27], (DEPTH, HY_FILT_HIDDEN)),
        "hy_filt_w3": nrm(ks[28], (DEPTH, HY_FILT_HIDDEN, N_DIR * HY_WIDTH), 0.1 * HY_FILT_HIDDEN ** -0.5),
        "hy_bias": nrm(ks[29], (DEPTH, HY_WIDTH), 1.0),
        "mix_norm_g": gain(ks[30], (DEPTH, D_MIX)),
        "w_out": nrm(ks[31], (DEPTH, D_MIX, D_MODEL), D_MIX ** -0.5),
        "norm2_g": gain(ks[32], (DEPTH, D_MODEL)),
        "w_up": nrm(ks[33], (DEPTH, D_MODEL, 2 * D_FF), D_MODEL ** -0.5),
        "ffn_conv_w": nrm(ks[34], (DEPTH, FFN_CONV, 2 * D_FF), FFN_CONV ** -0.5),
        "ffn_conv_b": nrm(ks[35], (DEPTH, 2 * D_FF), 0.01),
        "w_down": nrm(ks[36], (DEPTH, D_FF, D_MODEL), D_FF ** -0.5),
        "final_norm_g": gain(ks[37], (D_MODEL,)),
    }


def reference(x, norm1_g, w_in, rg_conv_w, rg_conv_b, rg_wa, rg_ba, rg_wx, rg_bx, rg_lambda,
              s5_a_re, s5_a_im, s5_log_dt, s5_b_re, s5_b_im, s5_c_re, s5_c_im, s5_d, s5_glu_w, s5_glu_b,
              hy_conv_w, hy_conv_b, hy_filt_w1, hy_filt_b1, hy_filt_freq1, hy_filt_w2, hy_filt_b2,
              hy_filt_freq2, hy_filt_w3, hy_bias, mix_norm_g, w_out,
              norm2_g, w_up, ffn_conv_w, ffn_conv_b, w_down, final_norm_g):
    h = x.astype(jnp.float32)
    split_idx = [RG_WIDTH, 2 * RG_WIDTH, 2 * RG_WIDTH + S5_WIDTH]
    for l in range(DEPTH):
        n = rmsnorm(h, norm1_g[l])
        proj = n @ w_in[l]
        xa, ga, ub, pc = jnp.split(proj, split_idx, axis=-1)
        ya = rglru_mixer(xa, ga, rg_conv_w[l], rg_conv_b[l], rg_wa[l], rg_ba[l],
                         rg_wx[l], rg_bx[l], rg_lambda[l])
        yb = s5_mixer(ub, s5_a_re[l], s5_a_im[l], s5_log_dt[l], s5_b_re[l], s5_b_im[l],
                      s5_c_re[l], s5_c_im[l], s5_d[l], s5_glu_w[l], s5_glu_b[l])
        yc = hyena_mixer(pc, hy_conv_w[l], hy_conv_b[l], hy_filt_w1[l], hy_filt_b1[l], hy_filt_freq1[l],
                         hy_filt_w2[l], hy_filt_b2[l], hy_filt_freq2[l], hy_filt_w3[l], hy_bias[l])
        ymix = jnp.concatenate([rms_nogain(ya), rms_nogain(yb), rms_nogain(yc)], axis=-1) * mix_norm_g[l]
        h = h + ymix @ w_out[l]
        h = h + conv_ffn(rmsnorm(h, norm2_g[l]), w_up[l], ffn_conv_w[l], ffn_conv_b[l], w_down[l])
    return rmsnorm(h, final_norm_g).astype(x.dtype)
```

```python
import math
from contextlib import ExitStack
import numpy as np
import ml_dtypes
import concourse.bass as bass
import concourse.mybir as mybir
from concourse.bass_utils import run_bass_kernel_spmd

F32 = mybir.dt.float32
BF16 = mybir.dt.bfloat16
AF = mybir.ActivationFunctionType
ALU = mybir.AluOpType

D = 1024; L = 4096; DEPTH = 2; NB = 4
RGW = 384; S5W = 384; HYW = 256; INC = 1920; DFF = 2816
EPS = 1e-6
TWO_PI = 2.0 * math.pi
MAGIC = 12582912.0
NFFT = 8192


class Res:
    __slots__ = ("lastw", "readers", "name")

    def __init__(self, name=""):
        self.lastw = None
        self.readers = {}
        self.name = name


class KB:
    NDMA = 12
    NPOOL = 4
    EPOCH_LIMIT = 3000

    def __init__(self, nc):
        self.nc = nc
        self.eng = {"pe": nc.tensor, "act": nc.scalar, "dve": nc.vector, "pool": nc.gpsimd, "sp": nc.sync}
        self.epoch = 0
        self.nsem = 0
        self._new_sems()
        self.drr = 0
        self.nins = 0
        self.rec = {e: [] for e in self.eng}
        self.use_block = False

    def flush(self):
        if not any(self.rec.values()):
            return
        rec = self.rec
        self.rec = {e: [] for e in self.eng}

        def replay(items):
            def f(engine):
                for it in items:
                    if it[0] == "w":
                        engine.wait_ge(it[1], it[2])
                    else:
                        ins = it[1](engine)
                        if it[2] is not None:
                            ins.then_inc(it[2], it[3])
            return f
        if not self.use_block:
            for e in ("sp", "pe", "act", "dve", "pool"):
                replay(rec[e])(self.eng[e])
            return
        with self.nc.Block() as block:
            if rec["sp"]:
                block.sync(replay(rec["sp"]))
            if rec["pe"]:
                block.tensor(replay(rec["pe"]))
            if rec["act"]:
                block.scalar(replay(rec["act"]))
            if rec["dve"]:
                block.vector(replay(rec["dve"]))
            if rec["pool"]:
                block.gpsimd(replay(rec["pool"]))

    def _new_sems(self):
        nc = self.nc
        self.sem = {}
        for e in self.eng:
            self.sem[e] = nc.alloc_semaphore("s%d_%s" % (self.epoch, e))
        self.dsem = [nc.alloc_semaphore("sd%d_%d" % (self.epoch, i)) for i in range(self.NDMA + self.NPOOL)]
        self.nsem += len(self.eng) + self.NDMA + self.NPOOL
        self.cnt = {e: 0 for e in self.eng}
        self.seen = {e: {} for e in self.eng}
        self.dcnt = [0] * (self.NDMA + self.NPOOL)
        self.prr = 0

    def _semh(self, key):
        return self.sem[key] if isinstance(key, str) else self.dsem[key[1]]

    def _wait(self, e, key, val, epoch=None):
        if epoch is not None and epoch < self.epoch:
            return
        if val <= 0 or self.seen[e].get(key, 0) >= val:
            return
        self.rec[e].append(("w", self._semh(key), val))
        self.seen[e][key] = val

    def _deps(self, e, reads, writes):
        for r in reads:
            if r.lastw is not None:
                k, v, ep = r.lastw
                if not (k == e and e == "pe"):
                    self._wait(e, k, v, ep)
        for w in writes:
            if w.lastw is not None:
                k, v, ep = w.lastw
                if k != e:
                    self._wait(e, k, v, ep)
            for k, (v, ep) in w.readers.items():
                if k != e:
                    self._wait(e, k, v, ep)

    def _mark(self, ev, reads, writes):
        k, v = ev
        for r in reads:
            old = r.readers.get(k)
            if old is None or old[1] < self.epoch or old[0] < v:
                r.readers[k] = (v, self.epoch)
        for w in writes:
            w.lastw = (k, v, self.epoch)
            w.readers = {}

    def op(self, e, fn, reads=(), writes=(), inc=True):
        self._deps(e, reads, writes)
        self.nins += 1
        if inc:
            self.cnt[e] += 1
            self.rec[e].append(("o", fn, self.sem[e], 1))
            ev = (e, self.cnt[e])
        else:
            self.rec[e].append(("o", fn, None, 0))
            ev = (e, self.cnt[e] + 1)
        self._mark(ev, reads, writes)

    def dma(self, q, out, in_, reads=(), writes=(), **kw):
        if q == "pool":
            i = self.NDMA + self.prr
            self.prr = (self.prr + 1) % self.NPOOL
        else:
            i = self.drr
            self.drr = (self.drr + 1) % self.NDMA
        self._wait(q, ("d", i), self.dcnt[i])
        self._deps(q, reads, writes)
        self.dcnt[i] += 16
        self.rec[q].append(("o", (lambda e, out=out, in_=in_, kw=kw: e.dma_start(out=out, in_=in_, **kw)), self.dsem[i], 16))
        self.nins += 1
        self._mark((("d", i), self.dcnt[i]), reads, writes)

    def barrier(self):
        for e in self.eng:
            for e2 in self.eng:
                self._wait(e, e2, self.cnt[e2])
            for i in range(self.NDMA + self.NPOOL):
                self._wait(e, ("d", i), self.dcnt[i])
        self.flush()
        if max(self.cnt.values()) > self.EPOCH_LIMIT or max(self.dcnt) > self.EPOCH_LIMIT:
            self.epoch += 1
            self._new_sems()


_CONST = {}


def _consts():
    if _CONST:
        return _CONST
    t = np.arange(L, dtype=np.int64)
    prod = (t[:, None] * t[None, :]) % NFFT
    ang = prod.astype(np.float64) * (TWO_PI / NFFT)
    C = np.cos(ang)
    S = np.sin(ang)
    _CONST["ctab"] = C.astype(ml_dtypes.bfloat16)
    _CONST["stab"] = S.astype(ml_dtypes.bfloat16)
    del C, S, ang, prod
    pos = np.arange(L, dtype=np.float32)
    tt = pos / np.float32(L - 1)
    w = (np.float32(2.0 * math.pi / L) * pos).astype(np.float32)
    bands = np.linspace(1e-4, 15, 16, dtype=np.float32)
    angf = w[:, None] * bands
    feats = np.concatenate([tt[:, None], np.cos(angf), -np.sin(angf)], axis=-1).astype(np.float32)
    _CONST["featsT"] = np.ascontiguousarray(feats.T)
    maxd = math.log(1e-2) / 0.3
    mind = math.log(1e-2) / 1.5
    deltas = np.abs(np.linspace(mind, maxd, HYW, dtype=np.float32))
    dec = np.exp(-tt[:, None] * np.tile(deltas, 2)).astype(np.float32)
    _CONST["decay"] = dec
    _CONST["ident"] = np.eye(128, dtype=np.float32)
    p = np.arange(128)
    _CONST["mask8"] = (p[:, None] // 16 == np.arange(8)[None, :]).astype(np.float32)
    m2 = (p[:, None] // 64 == np.arange(2)[None, :]).astype(np.float32)
    _CONST["mask2"] = np.concatenate([m2, -m2], axis=1).astype(np.float32)
    _CONST["iota"] = np.tile(np.arange(520, dtype=np.float32)[None, :], (128, 1))
    _CONST["maskq"] = (p[:, None] // 32 == np.arange(4)[None, :]).astype(np.float32)
    _CONST["alt"] = np.tile(np.where(np.arange(512) % 2 == 0, 1.0, -1.0).astype(np.float32)[None, :], (128, 1))
    _CONST["altc"] = np.where(np.arange(128) % 2 == 0, 1.0, -1.0).astype(np.float32)[:, None].copy()
    return _CONST


WEIGHT_NAMES = [
    "norm1_g", "w_in", "rg_conv_w", "rg_conv_b", "rg_wa", "rg_ba", "rg_wx", "rg_bx", "rg_lambda",
    "s5_a_re", "s5_a_im", "s5_log_dt", "s5_b_re", "s5_b_im", "s5_c_re", "s5_c_im", "s5_d", "s5_glu_w",
    "s5_glu_b", "hy_conv_w", "hy_conv_b", "hy_filt_w1", "hy_filt_b1", "hy_filt_freq1", "hy_filt_w2",
    "hy_filt_b2", "hy_filt_freq2", "hy_filt_w3", "hy_bias", "mix_norm_g", "w_out", "norm2_g", "w_up",
    "ffn_conv_w", "ffn_conv_b", "w_down", "final_norm_g"]

WEIGHT_SHAPES = {
    "norm1_g": (2, 1024), "w_in": (2, 1024, 1920), "rg_conv_w": (2, 4, 384), "rg_conv_b": (2, 384),
    "rg_wa": (2, 2, 6, 64, 64), "rg_ba": (2, 2, 384), "rg_wx": (2, 2, 6, 64, 64), "rg_bx": (2, 2, 384),
    "rg_lambda": (2, 2, 384), "s5_a_re": (2, 2, 24, 64), "s5_a_im": (2, 2, 24, 64), "s5_log_dt": (2, 2, 24),
    "s5_b_re": (2, 2, 24, 64, 16), "s5_b_im": (2, 2, 24, 64, 16), "s5_c_re": (2, 2, 24, 16, 64),
    "s5_c_im": (2, 2, 24, 16, 64), "s5_d": (2, 384), "s5_glu_w": (2, 384, 384), "s5_glu_b": (2, 384),
    "hy_conv_w": (2, 3, 768), "hy_conv_b": (2, 768), "hy_filt_w1": (2, 33, 64), "hy_filt_b1": (2, 64),
    "hy_filt_freq1": (2, 64), "hy_filt_w2": (2, 64, 64), "hy_filt_b2": (2, 64), "hy_filt_freq2": (2, 64),
    "hy_filt_w3": (2, 64, 512), "hy_bias": (2, 256), "mix_norm_g": (2, 1024), "w_out": (2, 1024, 1024),
    "norm2_g": (2, 1024), "w_up": (2, 1024, 5632), "ffn_conv_w": (2, 3, 5632), "ffn_conv_b": (2, 5632),
    "w_down": (2, 2816, 1024), "final_norm_g": (1024,)}

CONST_SHAPES = {"ctab": ((L, L), BF16), "stab": ((L, L), BF16), "featsT": ((33, L), F32),
                "decay": ((L, 512), F32), "ident": ((128, 128), F32), "mask8": ((128, 8), F32),
                "mask2": ((128, 4), F32), "iota": ((128, 520), F32),
                "alt": ((128, 512), F32), "altc": ((128, 1), F32),
                "maskq": ((128, 4), F32)}


def dap(t, offset, dims):
    return bass.AP(t.tensor, t.offset + offset, [list(d) for d in dims])


class Builder:
    def __init__(self, debug=False, phases=None, depth=DEPTH):
        self.debug = debug
        self.depth = depth
        self.phases = phases
        nc = bass.Bass("TRN2", target_bir_lowering=False)
        self.nc = nc
        self.k = KB(nc)
        self.W = {}
        self.x = nc.dram_tensor("x", [L, D], F32, kind="ExternalInput").ap()
        for n in WEIGHT_NAMES:
            self.W[n] = nc.dram_tensor(n, list(WEIGHT_SHAPES[n]), F32, kind="ExternalInput").ap()
        self.C = {}
        for n, (shp, dt) in CONST_SHAPES.items():
            self.C[n] = nc.dram_tensor(n, list(shp), dt, kind="ExternalInput").ap()
        self.out = nc.dram_tensor("out", [L, D], F32, kind="ExternalOutput").ap()
        skind = "ExternalOutput"
        self.hA = nc.dram_tensor("hA", [D, L], F32, kind=skind).ap()
        self.hB = nc.dram_tensor("hB", [D, L], F32, kind=skind).ap()
        self.projT = nc.dram_tensor("projT", [INC, L], F32, kind=skind).ap()
        self.ymixT = nc.dram_tensor("ymixT", [D, L], F32, kind=skind).ap()
        self.ps = [nc.alloc_psum_tensor("ps%d" % i, [128, 512], F32).ap() for i in range(8)]
        self.psr = [Res("ps%d" % i) for i in range(8)]
        self.ident = nc.alloc_sbuf_tensor("ident_sb", [128, 128], F32).ap()
        self.ones = nc.alloc_sbuf_tensor("ones_sb", [128, 128], BF16).ap()
        self.rg = Res("glob")
        k = self.k
        k.dma("sp", self.ident, self.C["ident"], writes=[self.rg])
        k.op("dve", lambda e: e.memset(self.ones, 1.0), writes=[self.rg])
        self._uid = 0

    def sb(self, es, shape, dt, name=None):
        self._uid += 1
        t = es.enter_context(self.nc.sbuf_tensor("%s_%d" % (name or "t", self._uid), list(shape), dt))
        return t.ap() if hasattr(t, "ap") else t

    def want(self, ph):
        return self.phases is None or ph in self.phases

    def rms_rstd(self, sq_tiles, n, width, psi, rstd, r_sq, r_rstd):
        k = self.k
        ps = self.ps[psi][:, 0:n]
        for i, s in enumerate(sq_tiles):
            k.op("pe", lambda e, s=s, i=i: e.matmul(ps, lhsT=self.ones, rhs=s, start=(i == 0),
                                                     stop=(i == len(sq_tiles) - 1)),
                 reads=[r_sq, self.rg], writes=[self.psr[psi]], inc=(i == len(sq_tiles) - 1))
        k.op("act", lambda e: e.activation(out=rstd, in_=ps, func=AF.Sqrt, scale=1.0 / width, bias=self.epsb),
             reads=[self.psr[psi], self.rg], writes=[r_rstd])
        k.op("dve", lambda e: e.reciprocal(out=rstd, in_=rstd), reads=[r_rstd], writes=[r_rstd])

    def build(self):
        k = self.k
        nc = self.nc
        self.epsb = nc.alloc_sbuf_tensor("epsb", [128, 1], F32).ap()
        k.op("dve", lambda e: e.memset(self.epsb, EPS), writes=[self.rg])
        self.oneb = nc.alloc_sbuf_tensor("oneb", [128, 1], F32).ap()
        k.op("dve", lambda e: e.memset(self.oneb, 1.0), writes=[self.rg])
        self.halfpi = nc.alloc_sbuf_tensor("halfpi", [128, 1], F32).ap()
        k.op("dve", lambda e: e.memset(self.halfpi, math.pi / 2), writes=[self.rg])
        if self.want("p0"):
            self.phase0()
        k.barrier()
        for l in range(self.depth):
            if self.want("p1"):
                self.phase1(l)
            k.barrier()
            if self.want("rg"):
                self.phase_rg(l)
            k.barrier()
            if self.want("s5"):
                self.phase_s5(l)
            k.barrier()
            if self.want("hy"):
                self.phase_hy(l)
            k.barrier()
            if self.want("p3"):
                self.phase3(l)
            k.barrier()
            if self.want("p4"):
                self.phase4(l)
            k.barrier()
        if self.want("pf"):
            self.phase_final()
        k.barrier()
        return nc

    def phase0(self):
        k = self.k
        hv = self.hA.rearrange("(kt p) t -> p kt t", p=128)
        with ExitStack() as es:
            xt = [self.sb(es, [128, D], F32, "xt") for _ in range(2)]
            st = [self.sb(es, [128, 8, 128], F32, "st") for _ in range(2)]
            rx = [Res() for _ in range(2)]
            rs = [Res() for _ in range(2)]
            for tt in range(32):
                b = tt % 2
                k.dma("sp", xt[b], self.x[tt * 128:(tt + 1) * 128, :], writes=[rx[b]])
                for half in range(2):
                    pi = (tt * 2 + half) % 4
                    for j in range(4):
                        kt = half * 4 + j
                        k.op("pe", lambda e, kt=kt, j=j, pi=pi, b=b: e.transpose(
                            out=self.ps[pi][:, j * 128:(j + 1) * 128], in_=xt[b][:, kt * 128:(kt + 1) * 128],
                            identity=self.ident), reads=[rx[b], self.rg], writes=[self.psr[pi]], inc=(j == 3))
                    eng = "act" if half == 0 else "dve"
                    src = self.ps[pi].rearrange("p (j t) -> p j t", j=4)
                    if eng == "act":
                        k.op("act", lambda e, src=src, half=half, b=b: e.activation(
                            out=st[b][:, half * 4:half * 4 + 4, :], in_=src, func=AF.Copy),
                            reads=[self.psr[pi]], writes=[rs[b]])
                    else:
                        k.op("dve", lambda e, src=src, half=half, b=b: e.tensor_copy(
                            out=st[b][:, half * 4:half * 4 + 4, :], in_=src),
                            reads=[self.psr[pi]], writes=[rs[b]])
                k.dma("sp", hv[:, :, tt * 128:(tt + 1) * 128], st[b], reads=[rs[b]])
            k.barrier()

    def load_gain(self, es, g_ap, name):
        k = self.k
        g = self.sb(es, [128, 8], F32, name)
        r = Res(name)
        k.dma("sp", g, g_ap.rearrange("(kt p) -> p kt", p=128), writes=[r], allow_slow_non_contiguous=True)
        return g, r

    def norm_tile(self, ht, n, gain, r_gain, sq, nb, rstd, r_ht, r_sq, r_nb, r_rstd, psi, ntiles=8, out_dt_bf=True):
        k = self.k
        k.op("act", lambda e: e.activation(out=sq[:, :, 0:n], in_=ht[:, :, 0:n], func=AF.Square),
             reads=[r_ht], writes=[r_sq])
        self.rms_rstd([sq[:, kt, 0:n] for kt in range(ntiles)], n, float(D), psi, rstd[:, 0:n], r_sq, r_rstd)
        for kt in range(ntiles):
            k.op("dve", lambda e, kt=kt: e.scalar_tensor_tensor(
                out=nb[:, kt, 0:n], in0=ht[:, kt, 0:n], scalar=gain[:, kt:kt + 1], in1=rstd[:, 0:n],
                op0=ALU.mult, op1=ALU.mult), reads=[r_ht, r_rstd, r_gain], writes=[r_nb])

    def phase1(self, l):
        k = self.k
        hv = self.hA.rearrange("(kt p) t -> p kt t", p=128)
        pv = self.projT.rearrange("(ct p) t -> p ct t", p=128)
        with ExitStack() as es:
            wi = self.sb(es, [128, 8, INC], BF16, "wi")
            r_wi = Res("wi")
            wv = self.W["w_in"][l].rearrange("(kt p) c -> p kt c", p=128)
            for kt in range(8):
                k.dma("pool", wi[:, kt, :], wv[:, kt, :], writes=[r_wi])
            g1, r_g1 = self.load_gain(es, self.W["norm1_g"][l], "g1")
            ht = [self.sb(es, [128, 8, 512], F32, "ht") for _ in range(2)]
            r_ht = [Res() for _ in range(2)]
            sq = self.sb(es, [128, 8, 512], BF16, "sq"); r_sq = Res()
            nb = self.sb(es, [128, 8, 512], BF16, "nb"); r_nb = Res()
            rstd = self.sb(es, [128, 512], F32, "rstd"); r_rstd = Res()
            so = self.sb(es, [128, 15, 512], F32, "so"); r_so = Res()
            for tt in range(8):
                b = tt % 2
                k.dma("sp", ht[b], hv[:, :, tt * 512:(tt + 1) * 512], writes=[r_ht[b]])
                self.norm_tile(ht[b], 512, g1, r_g1, sq, nb, rstd, r_ht[b], r_sq, r_nb, r_rstd, psi=7)
                for ct in range(15):
                    pi = ct % 6
                    for kt in range(8):
                        k.op("pe", lambda e, ct=ct, kt=kt, pi=pi: e.matmul(
                            self.ps[pi], lhsT=wi[:, kt, ct * 128:(ct + 1) * 128], rhs=nb[:, kt, :],
                            start=(kt == 0), stop=(kt == 7)), reads=[r_wi, r_nb], writes=[self.psr[pi]],
                            inc=(kt == 7))
                    if ct % 2 == 0:
                        k.op("act", lambda e, ct=ct, pi=pi: e.activation(out=so[:, ct, :], in_=self.ps[pi],
                                                                            func=AF.Copy),
                             reads=[self.psr[pi]], writes=[r_so])
                    else:
                        k.op("dve", lambda e, ct=ct, pi=pi: e.tensor_copy(out=so[:, ct, :], in_=self.ps[pi]),
                             reads=[self.psr[pi]], writes=[r_so])
                k.dma("sp", pv[:, :, tt * 512:(tt + 1) * 512], so, reads=[r_so])
            k.barrier()

    def dwconv(self, out, xp, left, K, wcol, bcol, r_xp, r_w, r_out, n=L, eng2="dve"):
        k = self.k
        k.op("act", lambda e: e.activation(out=out, in_=xp[:, left:left + n], func=AF.Identity,
                                            scale=wcol(left), bias=bcol),
             reads=[r_xp, r_w], writes=[r_out])
        for j in range(K):
            if j == left:
                continue
            k.op(eng2, lambda e, j=j: e.scalar_tensor_tensor(out=out, in0=xp[:, j:j + n], scalar=wcol(j), in1=out,
                                                              op0=ALU.mult, op1=ALU.add),
                 reads=[r_xp, r_w, r_out], writes=[r_out])

    def phase_rg(self, l):
        k = self.k
        W = self.W
        for j in range(3):
            with ExitStack() as es:
                cs = slice(j * 128, (j + 1) * 128)
                par = self.sb(es, [128, 16], F32, "rgpar"); r_par = Res()
                k.dma("sp", par[:, 0:4], W["rg_conv_w"][l][:, cs].rearrange("k c -> c k"), writes=[r_par],
                      allow_slow_non_contiguous=True)
                k.dma("sp", par[:, 4:5], W["rg_conv_b"][l][cs].rearrange("(c o) -> c o", o=1), writes=[r_par],
                      allow_slow_non_contiguous=True)
                for (c0, nm) in ((5, "rg_ba"), (7, "rg_bx"), (9, "rg_lambda")):
                    k.dma("sp", par[:, c0:c0 + 2], W[nm][l][:, cs].rearrange("d c -> c d"), writes=[r_par],
                          allow_slow_non_contiguous=True)
                tmp = self.sb(es, [128, 8], F32, "rgtmp"); r_tmp = Res()
                k.op("act", lambda e: e.activation(out=tmp[:, 0:2], in_=par[:, 9:11], func=AF.Exp, scale=-1.0),
                     reads=[r_par], writes=[r_tmp])
                k.op("dve", lambda e: e.tensor_scalar(out=tmp[:, 2:4], in0=tmp[:, 0:2], scalar1=2.0, scalar2=None,
                                                       op0=ALU.add), reads=[r_tmp], writes=[r_tmp])
                k.op("dve", lambda e: e.reciprocal(out=tmp[:, 2:4], in_=tmp[:, 2:4]), reads=[r_tmp], writes=[r_tmp])
                k.op("dve", lambda e: e.tensor_tensor(out=tmp[:, 0:2], in0=tmp[:, 0:2], in1=tmp[:, 2:4], op=ALU.mult),
                     reads=[r_tmp], writes=[r_tmp])
                k.op("dve", lambda e: e.tensor_tensor(out=tmp[:, 2:4], in0=tmp[:, 0:2], in1=tmp[:, 0:2], op=ALU.mult),
                     reads=[r_tmp], writes=[r_tmp])
                k.op("dve", lambda e: e.tensor_scalar(out=tmp[:, 4:6], in0=tmp[:, 2:4], scalar1=1.0 / 7, scalar2=1.0 / 5,
                                                       op0=ALU.mult, op1=ALU.add), reads=[r_tmp], writes=[r_tmp])
                k.op("dve", lambda e: e.tensor_tensor(out=tmp[:, 4:6], in0=tmp[:, 4:6], in1=tmp[:, 2:4], op=ALU.mult),
                     reads=[r_tmp], writes=[r_tmp])
                k.op("dve", lambda e: e.tensor_scalar(out=tmp[:, 4:6], in0=tmp[:, 4:6], scalar1=1.0 / 3, scalar2=None,
                                                       op0=ALU.add), reads=[r_tmp], writes=[r_tmp])
                k.op("dve", lambda e: e.tensor_tensor(out=tmp[:, 4:6], in0=tmp[:, 4:6], in1=tmp[:, 2:4], op=ALU.mult),
                     reads=[r_tmp], writes=[r_tmp])
                k.op("dve", lambda e: e.tensor_scalar(out=tmp[:, 4:6], in0=tmp[:, 4:6], scalar1=1.0, scalar2=None,
                                                       op0=ALU.add), reads=[r_tmp], writes=[r_tmp])
                k.op("dve", lambda e: e.tensor_tensor(out=tmp[:, 4:6], in0=tmp[:, 4:6], in1=tmp[:, 0:2], op=ALU.mult),
                     reads=[r_tmp], writes=[r_tmp])
                k.op("dve", lambda e: e.tensor_scalar(out=par[:, 11:13], in0=tmp[:, 4:6], scalar1=-16.0, scalar2=None,
                                                       op0=ALU.mult), reads=[r_tmp, r_par], writes=[r_par])
                wg = self.sb(es, [128, 4, 128], BF16, "rgw"); r_wg = Res()
                k.op("pool", lambda e: e.memset(wg, 0.0), writes=[r_wg])
                for d in range(2):
                    for gi, nm in enumerate(("rg_wa", "rg_wx")):
                        for hh in range(2):
                            k.dma("pool", wg[hh * 64:(hh + 1) * 64, d * 2 + gi, hh * 64:(hh + 1) * 64],
                                  W[nm][l, d, 2 * j + hh], writes=[r_wg])
                xp = self.sb(es, [128, L + 3], F32, "rgxp"); r_xp = Res()
                ga = self.sb(es, [128, L], F32, "rgga"); r_ga = Res()
                u = self.sb(es, [128, L], F32, "rgu"); r_u = Res()
                ub = self.sb(es, [128, L], BF16, "rgub"); r_ub = Res()
                Rb = self.sb(es, [128, L], F32, "rgR"); r_R = Res()
                Ib = self.sb(es, [128, L], F32, "rgI"); r_I = Res()
                Tb = self.sb(es, [128, L], F32, "rgT"); r_T = Res()
                H = [self.sb(es, [128, L], F32, "rgH") for _ in range(2)]
                r_H = [Res(), Res()]
                k.op("dve", lambda e: e.memset(xp[:, 0:1], 0.0), writes=[r_xp])
                k.op("dve", lambda e: e.memset(xp[:, L + 1:L + 3], 0.0), writes=[r_xp])
                k.dma("sp", xp[:, 1:L + 1], self.projT[j * 128:(j + 1) * 128, :], writes=[r_xp])
                k.dma("sp", ga, self.projT[RGW + j * 128:RGW + (j + 1) * 128, :], writes=[r_ga])
                self.dwconv(u, xp, 1, 4, lambda jj: par[:, jj:jj + 1], par[:, 4:5], r_xp, r_par, r_u)
                k.op("pool", lambda e: e.tensor_copy(out=ub, in_=u), reads=[r_u], writes=[r_ub])
                for d in range(2):
                    for gi, (dst, r_dst, bc) in enumerate(((Rb, r_R, 5), (Ib, r_I, 7))):
                        for ch in range(8):
                            pi = ch % 4 + 4 * gi
                            k.op("pe", lambda e, ch=ch, pi=pi, gi=gi, d=d: e.matmul(
                                self.ps[pi], lhsT=wg[:, d * 2 + gi, :], rhs=ub[:, ch * 512:(ch + 1) * 512],
                                start=True, stop=True), reads=[r_wg, r_ub], writes=[self.psr[pi]])
                            k.op("act", lambda e, ch=ch, pi=pi, dst=dst, bc=bc, d=d: e.activation(
                                out=dst[:, ch * 512:(ch + 1) * 512], in_=self.ps[pi], func=AF.Sigmoid,
                                bias=par[:, bc + d:bc + d + 1]), reads=[self.psr[pi], r_par], writes=[r_dst])
                    k.op("act", lambda e, d=d: e.activation(out=Rb, in_=Rb, func=AF.Exp, scale=par[:, 11 + d:12 + d]),
                         reads=[r_R, r_par], writes=[r_R])
                    k.op("act", lambda e: e.activation(out=Tb, in_=Rb, func=AF.Square), reads=[r_R], writes=[r_T])
                    k.op("act", lambda e: e.activation(out=Tb, in_=Tb, func=AF.Sqrt, scale=-1.0, bias=self.oneb),
                         reads=[r_T, self.rg], writes=[r_T])
                    k.op("dve", lambda e: e.tensor_tensor(out=Ib, in0=Ib, in1=Tb, op=ALU.mult), reads=[r_I, r_T],
                         writes=[r_I])
                    k.op("dve", lambda e: e.tensor_tensor(out=Ib, in0=Ib, in1=u, op=ALU.mult), reads=[r_I, r_u],
                         writes=[r_I])
                    if d == 0:
                        k.op("dve", lambda e: e.tensor_tensor_scan(out=H[0], data0=Rb, data1=Ib, initial=0.0,
                                                                    op0=ALU.mult, op1=ALU.add),
                             reads=[r_R, r_I], writes=[r_H[0]])
                    else:
                        k.op("dve", lambda e: e.tensor_tensor_scan(out=H[1][:, ::-1], data0=Rb[:, ::-1],
                                                                    data1=Ib[:, ::-1], initial=0.0,
                                                                    op0=ALU.mult, op1=ALU.add),
                             reads=[r_R, r_I], writes=[r_H[1]])
                k.op("act", lambda e: e.activation(out=ga, in_=ga, func=AF.Gelu_apprx_tanh), reads=[r_ga], writes=[r_ga])
                k.op("pool", lambda e: e.tensor_tensor(out=H[0], in0=H[0], in1=H[1], op=ALU.add),
                     reads=[r_H[0], r_H[1]], writes=[r_H[0]])
                k.op("dve", lambda e: e.tensor_tensor(out=H[0], in0=H[0], in1=ga, op=ALU.mult),
                     reads=[r_H[0], r_ga], writes=[r_H[0]])
                k.dma("sp", self.ymixT[j * 128:(j + 1) * 128, :], H[0], reads=[r_H[0]])
                k.barrier()

    def reduce_angle(self, out, src, tmp, r_src, r_out, r_tmp, eng="dve"):
        k = self.k
        k.op(eng, lambda e: e.tensor_scalar(out=tmp, in0=src, scalar1=1.0 / TWO_PI, scalar2=MAGIC, op0=ALU.mult,
                                            op1=ALU.add), reads=[r_src], writes=[r_tmp])
        k.op(eng, lambda e: e.tensor_scalar(out=tmp, in0=tmp, scalar1=MAGIC, scalar2=-TWO_PI, op0=ALU.subtract,
                                            op1=ALU.mult), reads=[r_tmp], writes=[r_tmp])
        k.op(eng, lambda e: e.tensor_tensor(out=out, in0=src, in1=tmp, op=ALU.add), reads=[r_src, r_tmp],
             writes=[r_out])
        k.op(eng, lambda e: e.tensor_scalar(out=out, in0=out, scalar1=-3.14159, scalar2=3.14159, op0=ALU.max,
                                            op1=ALU.min), reads=[r_out], writes=[r_out])

    def sincos(self, sin_out, cos_out, ang, tmp, r_ang, r_sin, r_cos, r_tmp):
        k = self.k
        k.op("act", lambda e: e.activation(out=sin_out, in_=ang, func=AF.Sin), reads=[r_ang], writes=[r_sin])
        k.op("act", lambda e: e.activation(out=tmp, in_=ang, func=AF.Abs), reads=[r_ang], writes=[r_tmp])
        k.op("act", lambda e: e.activation(out=cos_out, in_=tmp, func=AF.Sin, scale=-1.0, bias=self.halfpi),
             reads=[r_tmp, self.rg], writes=[r_cos])

    def phase_s5(self, l):
        k = self.k
        W = self.W
        with ExitStack() as es:
            r = Res("s5par")
            lrS = self.sb(es, [128, 24], F32, "lrS"); liS = self.sb(es, [128, 24], F32, "liS")
            ldS = self.sb(es, [128, 24], F32, "ldS")
            for d in range(2):
                k.dma("sp", lrS[:, d * 12:(d + 1) * 12], W["s5_a_re"][l, d].rearrange("(i g) p -> (g p) i", g=2),
                      writes=[r], allow_slow_non_contiguous=True)
                k.dma("sp", liS[:, d * 12:(d + 1) * 12], W["s5_a_im"][l, d].rearrange("(i g) p -> (g p) i", g=2),
                      writes=[r], allow_slow_non_contiguous=True)
                for g2 in range(2):
                    src = W["s5_log_dt"][l, d].rearrange("(i g) -> g i", g=2)[g2:g2 + 1, :]
                    k.dma("sp", ldS[g2 * 64:(g2 + 1) * 64, d * 12:(d + 1) * 12], src.to_broadcast([64, 12]),
                          writes=[r], allow_slow_non_contiguous=True)
            rhoS = self.sb(es, [128, 24], F32, "rhoS"); thS = self.sb(es, [128, 24], F32, "thS")
            tS = self.sb(es, [128, 24], F32, "tS"); th0 = self.sb(es, [128, 24], F32, "th0")
            k.op("act", lambda e: e.activation(out=ldS, in_=ldS, func=AF.Exp), reads=[r], writes=[r])
            k.op("dve", lambda e: e.tensor_tensor(out=rhoS, in0=lrS, in1=ldS, op=ALU.mult), reads=[r], writes=[r])
            k.op("act", lambda e: e.activation(out=rhoS, in_=rhoS, func=AF.Exp), reads=[r], writes=[r])
            k.op("dve", lambda e: e.tensor_tensor(out=th0, in0=liS, in1=ldS, op=ALU.mult), reads=[r], writes=[r])
            self.reduce_angle(thS, th0, tS, r, r, r)
            mask2 = self.sb(es, [128, 4], F32, "mask2")
            k.dma("sp", mask2, self.C["mask2"], writes=[r])
            maskq = self.sb(es, [128, 4], F32, "maskq")
            k.dma("sp", maskq, self.C["maskq"], writes=[r])
            LC = self.sb(es, [128, 2, 12, 2, 128], BF16, "LC"); r_LC = Res()
            k.op("pool", lambda e: e.memset(LC, 0.0), writes=[r_LC])
            LB = self.sb(es, [128, 3, 2, 2, 512], BF16, "LB"); r_LB = Res()
            with ExitStack() as es2:
                rb = Res("s5B")
                def st(nm):
                    return self.sb(es2, [128, 24], F32, nm)
                snS, csS, nrS, t1, t2, rdn, cfr, cfi = [st("s5s%d" % i) for i in range(8)]
                self.sincos(snS, csS, thS, t1, r, rb, rb, rb)
                def tt_(out, a_, b_, op, rd=(r, rb)):
                    k.op("dve", lambda e: e.tensor_tensor(out=out, in0=a_, in1=b_, op=op), reads=list(rd), writes=[rb])
                tt_(csS, csS, rhoS, ALU.mult)
                tt_(snS, snS, rhoS, ALU.mult)
                k.op("dve", lambda e: e.tensor_scalar(out=nrS, in0=csS, scalar1=-1.0, scalar2=None, op0=ALU.add),
                     reads=[rb], writes=[rb])
                tt_(t1, lrS, lrS, ALU.mult); tt_(t2, liS, liS, ALU.mult); tt_(rdn, t1, t2, ALU.add)
                k.op("dve", lambda e: e.reciprocal(out=rdn, in_=rdn), reads=[rb], writes=[rb])
                tt_(t1, nrS, lrS, ALU.mult); tt_(t2, snS, liS, ALU.mult); tt_(t1, t1, t2, ALU.add); tt_(cfr, t1, rdn, ALU.mult)
                tt_(t1, snS, lrS, ALU.mult); tt_(t2, nrS, liS, ALU.mult); tt_(t1, t1, t2, ALU.subtract); tt_(cfi, t1, rdn, ALU.mult)
                Bn = self.sb(es2, [128, 2, 12, 2, 16], F32, "Bn")
                BbS = self.sb(es2, [128, 2, 12, 2, 16], F32, "BbS")
                u1 = self.sb(es2, [128, 12, 16], F32, "s5u1"); u2 = self.sb(es2, [128, 12, 16], F32, "s5u2")
                for d in range(2):
                    for ri, nm in enumerate(("s5_b_re", "s5_b_im")):
                        k.dma("sp" if ri == 0 else "act", Bn[:, d, :, ri, :],
                              W[nm][l, d].rearrange("(i g) p c -> (g p) i c", g=2), writes=[rb])
                for d in range(2):
                    crb = cfr[:, d * 12:(d + 1) * 12].unsqueeze(2).to_broadcast([128, 12, 16])
                    cib = cfi[:, d * 12:(d + 1) * 12].unsqueeze(2).to_broadcast([128, 12, 16])
                    Br, Bi = Bn[:, d, :, 0, :], Bn[:, d, :, 1, :]
                    tt_(u1, crb, Br, ALU.mult); tt_(u2, cib, Bi, ALU.mult); tt_(BbS[:, d, :, 0, :], u1, u2, ALU.subtract)
                    tt_(u1, crb, Bi, ALU.mult); tt_(u2, cib, Br, ALU.mult); tt_(BbS[:, d, :, 1, :], u1, u2, ALU.add)
                W4 = [self.sb(es2, [128, 128], F32, "W4") for _ in range(2)]; r_W4 = [Res(), Res()]
                Cn4 = self.sb(es2, [128, 12, 128], F32, "Cn4"); r_Cn = Res()
                k.op("pool", lambda e: e.memset(Cn4, 0.0), writes=[r_Cn])
                n = 0
                for ct in range(3):
                    for d in range(2):
                        for ri, nm in enumerate(("s5_c_re", "s5_c_im")):
                            slot = (ct * 2 + d) * 2 + ri
                            for q in range(4):
                                for g2 in range(2):
                                    k.dma("sp" if n % 2 == 0 else "act",
                                          Cn4[q * 32 + g2 * 16:q * 32 + g2 * 16 + 16, slot, g2 * 64:(g2 + 1) * 64],
                                          W[nm][l, d, 8 * ct + 2 * q + g2], writes=[r_Cn]); n += 1
                n = 0
                for ct in range(3):
                    for d in range(2):
                        for ri in range(2):
                            b = n % 2; pi = n % 4; n += 1
                            for q in range(4):
                                dst = W4[b][:, 32 * q:32 * q + 32].rearrange("p (g c) -> p g c", g=2)
                                k.op("dve", lambda e, dst=dst, d=d, ct=ct, ri=ri, q=q: e.tensor_tensor(
                                    out=dst, in0=BbS[:, d, ct * 4 + q, ri, :].unsqueeze(1).to_broadcast([128, 2, 16]),
                                    in1=mask2[:, 0:2].unsqueeze(2).to_broadcast([128, 2, 16]), op=ALU.mult),
                                    reads=[rb, r], writes=[r_W4[b]])
                            k.op("pe", lambda e, b=b, pi=pi: e.transpose(out=self.ps[pi][:, 0:128], in_=W4[b], identity=self.ident),
                                 reads=[r_W4[b], self.rg], writes=[self.psr[pi]])
                            for q in range(4):
                                k.op("dve", lambda e, ct=ct, d=d, ri=ri, q=q, pi=pi: e.tensor_scalar(
                                    out=LB[:, ct, d, ri, q * 128:(q + 1) * 128], in0=self.ps[pi][:, 0:128],
                                    scalar1=maskq[:, q:q + 1], scalar2=None, op0=ALU.mult),
                                    reads=[self.psr[pi], r], writes=[r_LB])
                            slot = (ct * 2 + d) * 2 + ri
                            pj = 4 + (n % 4)
                            k.op("pe", lambda e, slot=slot, pj=pj: e.transpose(out=self.ps[pj][:, 0:128], in_=Cn4[:, slot, :],
                                                                               identity=self.ident),
                                 reads=[r_Cn, self.rg], writes=[self.psr[pj]])
                            sgn = 1.0 if ri == 0 else -1.0
                            for q in range(4):
                                k.op("act", lambda e, ct=ct, d=d, ri=ri, q=q, pj=pj, sgn=sgn: e.activation(
                                    out=LC[:, d, ct * 4 + q, ri, 32 * q:32 * q + 32], in_=self.ps[pj][:, 32 * q:32 * q + 32],
                                    func=AF.Identity, scale=sgn), reads=[self.psr[pj]], writes=[r_LC])
                k.barrier()
            Y = self.sb(es, [128, 3, L], F32, "s5Y"); r_Y = [Res() for _ in range(3)]
            iota = self.sb(es, [128, 520], F32, "iota"); k.dma("sp", iota, self.C["iota"], writes=[r])
            onesf = self.sb(es, [128, 512], F32, "onesf"); k.op("pool", lambda e: e.memset(onesf, 1.0), writes=[r])
            dsk = self.sb(es, [128, 3], F32, "dsk")
            k.dma("sp", dsk, W["s5_d"][l].rearrange("(ct p) -> p ct", p=128), writes=[r], allow_slow_non_contiguous=True)
            with ExitStack() as es3:
                Ubf = self.sb(es3, [128, L], BF16, "Ubf"); r_Ubf = Res()
                U32 = self.sb(es3, [128, L], F32, "U32"); r_U32 = Res()
                TC = [self.sb(es3, [128, 513], F32, "TC") for _ in range(4)]
                TS = [self.sb(es3, [128, 513], F32, "TS") for _ in range(4)]
                RT = [self.sb(es3, [128, 512], F32, "RT") for _ in range(4)]
                r_tab = [Res() for _ in range(4)]
                carry = self.sb(es3, [128, 4, 4], F32, "carry"); r_car = [Res() for _ in range(4)]
                tA = self.sb(es3, [128, 513], F32, "tA"); tB = self.sb(es3, [128, 513], F32, "tB")
                r_tAB = Res()
                NT = 2
                tmp = [[self.sb(es3, [128, 512], F32, "s5tmp") for _ in range(8)] for _ in range(NT)]
                r_tmp = [[Res() for _ in range(8)] for _ in range(NT)]
                Hb = [[self.sb(es3, [128, 512], BF16, "s5hb") for _ in range(2)] for _ in range(NT)]
                r_Hb = [[Res() for _ in range(2)] for _ in range(NT)]
                it = 0
                for ct in range(3):
                    rows = slice(S5_OFF + ct * 128, S5_OFF + (ct + 1) * 128)
                    k.dma("pool", Ubf, self.projT[rows, :], writes=[r_Ubf])
                    k.dma("sp", U32, self.projT[rows, :], writes=[r_U32])
                    k.op("dve", lambda e, ct=ct: e.tensor_scalar(out=Y[:, ct, :], in0=U32, scalar1=dsk[:, ct:ct + 1],
                                                                  scalar2=None, op0=ALU.mult),
                         reads=[r_U32, r], writes=[r_Y[ct]])
                    for d in range(2):
                        for q in range(4):
                            i = ct * 4 + q
                            col = d * 12 + i
                            k.op("dve", lambda e, col=col: e.tensor_scalar(out=tA, in0=iota[:, 0:513],
                                                                            scalar1=thS[:, col:col + 1], scalar2=None,
                                                                            op0=ALU.mult), reads=[r], writes=[r_tAB])
                            self.reduce_angle(tA, tA, tB, r_tAB, r_tAB, r_tAB)
                            self.sincos(TS[q], TC[q], tA, tB, r_tAB, r_tab[q], r_tab[q], r_tAB)
                            k.op("dve", lambda e, q=q, col=col: e.tensor_scalar(
                                out=RT[q], in0=onesf, scalar1=rhoS[:, col:col + 1], scalar2=None, op0=ALU.mult),
                                reads=[r], writes=[r_tab[q]])
                            k.op("dve", lambda e, q=q: e.memset(carry[:, q, :], 0.0), writes=[r_car[q]])
                        order = range(8) if d == 0 else range(7, -1, -1)
                        for ch in order:
                            tsl = slice(ch * 512, (ch + 1) * 512)
                            pacc = 4 + (it % 2)
                            for q in range(4):
                                i = ct * 4 + q
                                pb = (q % 2) * 2
                                tm = tmp[q % NT]; rt = r_tmp[q % NT]
                                hb = Hb[q % NT]; rhb = r_Hb[q % NT]
                                for ri in range(2):
                                    k.op("pe", lambda e, ri=ri, pb=pb, ct=ct, d=d, q=q, tsl=tsl: e.matmul(
                                        self.ps[pb + ri], lhsT=LB[:, ct, d, ri, q * 128:(q + 1) * 128],
                                        rhs=Ubf[:, tsl], start=True, stop=True),
                                        reads=[r_LB, r_Ubf], writes=[self.psr[pb + ri]])
                                if d == 0:
                                    vr, vi = self.ps[pb], self.ps[pb + 1]
                                else:
                                    vr, vi = self.ps[pb][:, ::-1], self.ps[pb + 1][:, ::-1]
                                tc, ts = TC[q][:, 0:512], TS[q][:, 0:512]
                                rr = [self.psr[pb], self.psr[pb + 1], r_tab[q]]
                                k.op("dve", lambda e, tm=tm, tc=tc, vr=vr: e.tensor_tensor(out=tm[0], in0=tc, in1=vr, op=ALU.mult),
                                     reads=rr, writes=[rt[0]])
                                k.op("dve", lambda e, tm=tm, ts=ts, vi=vi: e.tensor_tensor(out=tm[1], in0=ts, in1=vi, op=ALU.mult),
                                     reads=rr, writes=[rt[1]])
                                k.op("dve", lambda e, tm=tm, tc=tc, vi=vi: e.tensor_tensor(out=tm[2], in0=tc, in1=vi, op=ALU.mult),
                                     reads=rr, writes=[rt[2]])
                                k.op("dve", lambda e, tm=tm, ts=ts, vr=vr: e.tensor_tensor(out=tm[3], in0=ts, in1=vr, op=ALU.mult),
                                     reads=rr, writes=[rt[3]])
                                k.op("pool", lambda e, tm=tm: e.tensor_tensor(out=tm[0], in0=tm[0], in1=tm[1], op=ALU.add),
                                     reads=[rt[0], rt[1]], writes=[rt[0]])
                                k.op("pool", lambda e, tm=tm: e.tensor_tensor(out=tm[2], in0=tm[2], in1=tm[3], op=ALU.subtract),
                                     reads=[rt[2], rt[3]], writes=[rt[2]])
                                k.op("dve", lambda e, tm=tm, q=q: e.tensor_tensor_scan(
                                    out=tm[4], data0=RT[q], data1=tm[0], initial=carry[:, q, 0:1], op0=ALU.mult, op1=ALU.add),
                                    reads=[rt[0], r_tab[q], r_car[q]], writes=[rt[4]])
                                k.op("dve", lambda e, tm=tm, q=q: e.tensor_tensor_scan(
                                    out=tm[5], data0=RT[q], data1=tm[2], initial=carry[:, q, 1:2], op0=ALU.mult, op1=ALU.add),
                                    reads=[rt[2], r_tab[q], r_car[q]], writes=[rt[5]])
                                c5, s5_ = TC[q][:, 512:513], TS[q][:, 512:513]
                                hrl, hil = tm[4][:, 511:512], tm[5][:, 511:512]
                                k.op("dve", lambda e, q=q, hil=hil, s5_=s5_: e.tensor_scalar(
                                    out=carry[:, q, 2:3], in0=hil, scalar1=s5_, scalar2=None, op0=ALU.mult),
                                    reads=[rt[5], r_tab[q]], writes=[r_car[q]])
                                k.op("dve", lambda e, q=q, hil=hil, c5=c5: e.tensor_scalar(
                                    out=carry[:, q, 3:4], in0=hil, scalar1=c5, scalar2=None, op0=ALU.mult),
                                    reads=[rt[5], r_tab[q]], writes=[r_car[q]])
                                k.op("dve", lambda e, q=q, hrl=hrl, c5=c5: e.scalar_tensor_tensor(
                                    out=carry[:, q, 0:1], in0=hrl, scalar=c5, in1=carry[:, q, 2:3], op0=ALU.mult,
                                    op1=ALU.subtract), reads=[rt[4], r_tab[q], r_car[q]], writes=[r_car[q]])
                                k.op("dve", lambda e, q=q, hrl=hrl, s5_=s5_: e.scalar_tensor_tensor(
                                    out=carry[:, q, 1:2], in0=hrl, scalar=s5_, in1=carry[:, q, 3:4], op0=ALU.mult,
                                    op1=ALU.add), reads=[rt[4], r_tab[q], r_car[q]], writes=[r_car[q]])
                                k.op("pool", lambda e, tm=tm, tc=tc: e.tensor_tensor(out=tm[6], in0=tc, in1=tm[4], op=ALU.mult),
                                     reads=[rt[4], r_tab[q]], writes=[rt[6]])
                                k.op("pool", lambda e, tm=tm, ts=ts: e.tensor_tensor(out=tm[7], in0=ts, in1=tm[5], op=ALU.mult),
                                     reads=[rt[5], r_tab[q]], writes=[rt[7]])
                                o0 = hb[0] if d == 0 else hb[0][:, ::-1]
                                o1 = hb[1] if d == 0 else hb[1][:, ::-1]
                                k.op("dve", lambda e, tm=tm, o0=o0: e.tensor_tensor(out=o0, in0=tm[6], in1=tm[7], op=ALU.subtract),
                                     reads=[rt[6], rt[7]], writes=[rhb[0]])
                                k.op("pool", lambda e, tm=tm, ts=ts: e.tensor_tensor(out=tm[6], in0=ts, in1=tm[4], op=ALU.mult),
                                     reads=[rt[4], r_tab[q], rhb[0]], writes=[rt[6]])
                                k.op("pool", lambda e, tm=tm, tc=tc: e.tensor_tensor(out=tm[7], in0=tc, in1=tm[5], op=ALU.mult),
                                     reads=[rt[5], r_tab[q], rhb[0]], writes=[rt[7]])
                                k.op("dve", lambda e, tm=tm, o1=o1: e.tensor_tensor(out=o1, in0=tm[6], in1=tm[7], op=ALU.add),
                                     reads=[rt[6], rt[7]], writes=[rhb[1]])
                                for ri in range(2):
                                    k.op("pe", lambda e, ri=ri, d=d, i=i, hb=hb, pacc=pacc, q=q: e.matmul(
                                        self.ps[pacc], lhsT=LC[:, d, i, ri, :], rhs=hb[ri], start=(q == 0 and ri == 0),
                                        stop=(q == 3 and ri == 1)), reads=[r_LC, rhb[ri]], writes=[self.psr[pacc]],
                                        inc=True)
                            k.op("dve", lambda e, ct=ct, tsl=tsl, pacc=pacc: e.tensor_tensor(
                                out=Y[:, ct, tsl], in0=Y[:, ct, tsl], in1=self.ps[pacc], op=ALU.add),
                                reads=[self.psr[pacc], r_Y[ct]], writes=[r_Y[ct]])
                            it += 1
                        k.barrier()
                k.barrier()
            with ExitStack() as es4:
                ygb = self.sb(es4, [128, 3, L], BF16, "ygb"); r_ygb = Res()
                gw = self.sb(es4, [128, 3, S5W], BF16, "gw"); r_gw = Res()
                gb = self.sb(es4, [128, 3], F32, "gb")
                sg = [self.sb(es4, [128, 512], F32, "sg") for _ in range(2)]; r_sg = [Res(), Res()]
                k.dma("pool", gw, W["s5_glu_w"][l].rearrange("(kt p) c -> p kt c", p=128), writes=[r_gw])
                k.dma("sp", gb, W["s5_glu_b"][l].rearrange("(ct p) -> p ct", p=128), writes=[r_gw],
                      allow_slow_non_contiguous=True)
                for ct in range(3):
                    k.op("act", lambda e, ct=ct: e.activation(out=Y[:, ct, :], in_=Y[:, ct, :], func=AF.Gelu_apprx_tanh),
                         reads=[r_Y[ct]], writes=[r_Y[ct]])
                    k.op("pool", lambda e, ct=ct: e.tensor_copy(out=ygb[:, ct, :], in_=Y[:, ct, :]), reads=[r_Y[ct]],
                         writes=[r_ygb])
                n = 0
                for ct in range(3):
                    for ch in range(8):
                        tsl = slice(ch * 512, (ch + 1) * 512)
                        pi = n % 4; b = n % 2; n += 1
                        for kt in range(3):
                            k.op("pe", lambda e, kt=kt, ct=ct, tsl=tsl, pi=pi: e.matmul(
                                self.ps[pi], lhsT=gw[:, kt, ct * 128:(ct + 1) * 128], rhs=ygb[:, kt, tsl],
                                start=(kt == 0), stop=(kt == 2)), reads=[r_gw, r_ygb], writes=[self.psr[pi]],
                                inc=(kt == 2))
                        k.op("act", lambda e, ct=ct, pi=pi, b=b: e.activation(out=sg[b], in_=self.ps[pi], func=AF.Sigmoid,
                                                                             bias=gb[:, ct:ct + 1]),
                             reads=[self.psr[pi], r_gw], writes=[r_sg[b]])
                        k.op("dve", lambda e, ct=ct, tsl=tsl, b=b: e.tensor_tensor(out=Y[:, ct, tsl], in0=Y[:, ct, tsl],
                                                                                  in1=sg[b], op=ALU.mult),
                             reads=[r_sg[b], r_Y[ct]], writes=[r_Y[ct]])
                    k.dma("sp", self.ymixT[RGW + ct * 128:RGW + (ct + 1) * 128, :], Y[:, ct, :], reads=[r_Y[ct]])
                k.barrier()


S5_OFF = 2 * RGW
HY_OFF = 2 * RGW + S5W


def _hy_methods():
    def phase_hy(self, l):
        k = self.k
        W = self.W
        P = HY_OFF
        with ExitStack() as es:
            zT = self.sb(es, [128, 2, L], F32, "zT"); r_zT = [Res(), Res()]
            U = self.sb(es, [128, 2, 2, L], F32, "hyU"); r_U = Res()
            hpar = self.sb(es, [128, 6, 4], F32, "hpar"); r_hp = Res()
            hbias = self.sb(es, [128, 2], F32, "hbias")
            esK = ExitStack()
            KZ = self.sb(esK, [128, 32, 768], BF16, "KZ"); r_KZ = Res()
            with ExitStack() as e1:
                r = Res("hyf")
                featsT = self.sb(e1, [33, L], F32, "featsT"); k.dma("sp", featsT, self.C["featsT"], writes=[r])
                w1 = self.sb(e1, [33, 64], F32, "w1"); k.dma("sp", w1, W["hy_filt_w1"][l], writes=[r])
                w2 = self.sb(e1, [64, 64], F32, "w2"); k.dma("sp", w2, W["hy_filt_w2"][l], writes=[r])
                w3 = self.sb(e1, [64, 512], F32, "w3"); k.dma("sp", w3, W["hy_filt_w3"][l], writes=[r])
                fp = self.sb(e1, [64, 8], F32, "fp")
                for c, nm in enumerate(("hy_filt_b1", "hy_filt_freq1", "hy_filt_b2", "hy_filt_freq2")):
                    k.dma("sp", fp[:, c:c + 1], W[nm][l].rearrange("(c o) -> c o", o=1), writes=[r],
                          allow_slow_non_contiguous=True)
                k.op("dve", lambda e: e.tensor_tensor(out=fp[:, 4:5], in0=fp[:, 0:1], in1=fp[:, 1:2], op=ALU.mult),
                     reads=[r], writes=[r])
                k.op("dve", lambda e: e.tensor_tensor(out=fp[:, 5:6], in0=fp[:, 2:3], in1=fp[:, 3:4], op=ALU.mult),
                     reads=[r], writes=[r])
                hid = [self.sb(e1, [64, L], F32, "hid") for _ in range(2)]; r_hid = [Res(), Res()]
                ta = self.sb(e1, [64, 512], F32, "hta"); tb = self.sb(e1, [64, 512], F32, "htb"); r_t = Res()
                for layer in range(2):
                    wm = w1 if layer == 0 else w2
                    for ch in range(8):
                        tsl = slice(ch * 512, (ch + 1) * 512)
                        pi = ch % 2
                        src = featsT[:, tsl] if layer == 0 else hid[0][:, tsl]
                        k.op("pe", lambda e, wm=wm, src=src, pi=pi: e.matmul(self.ps[pi][0:64, :], lhsT=wm, rhs=src,
                                                                               start=True, stop=True),
                             reads=[r, r_hid[0]], writes=[self.psr[pi]])
                        fc, bc = (1, 4) if layer == 0 else (3, 5)
                        k.op("dve", lambda e, pi=pi, fc=fc, bc=bc: e.tensor_scalar(
                            out=ta, in0=self.ps[pi][0:64, :], scalar1=fp[:, fc:fc + 1], scalar2=fp[:, bc:bc + 1],
                            op0=ALU.mult, op1=ALU.add), reads=[self.psr[pi], r], writes=[r_t])
                        self.reduce_angle(ta, ta, tb, r_t, r_t, r_t)
                        k.op("act", lambda e, layer=layer, tsl=tsl: e.activation(out=hid[layer][:, tsl], in_=ta, func=AF.Sin),
                             reads=[r_t], writes=[r_hid[layer]])
                dec = [self.sb(e1, [128, 512], F32, "dec") for _ in range(2)]; r_dec = [Res(), Res()]
                for tt in range(32):
                    b = tt % 2
                    pi = 2 + tt % 2
                    k.dma("sp", dec[b], self.C["decay"][tt * 128:(tt + 1) * 128, :], writes=[r_dec[b]])
                    k.op("pe", lambda e, tt=tt, pi=pi: e.matmul(self.ps[pi], lhsT=hid[1][:, tt * 128:(tt + 1) * 128], rhs=w3,
                                                                 start=True, stop=True),
                         reads=[r_hid[1], r], writes=[self.psr[pi]])
                    k.op("dve", lambda e, tt=tt, pi=pi, b=b: e.tensor_tensor(out=KZ[:, tt, 0:512], in0=dec[b], in1=self.ps[pi],
                                                                             op=ALU.mult),
                         reads=[self.psr[pi], r_dec[b]], writes=[r_KZ])
                k.barrier()
            for c6 in range(6):
                cs = slice(c6 * 128, (c6 + 1) * 128)
                k.dma("sp", hpar[:, c6, 0:3], W["hy_conv_w"][l][:, cs].rearrange("k c -> c k"), writes=[r_hp],
                      allow_slow_non_contiguous=True)
                k.dma("sp", hpar[:, c6, 3:4], W["hy_conv_b"][l][cs].rearrange("(c o) -> c o", o=1), writes=[r_hp],
                      allow_slow_non_contiguous=True)
            k.dma("sp", hbias, W["hy_bias"][l].rearrange("(m p) -> p m", p=128), writes=[r_hp], allow_slow_non_contiguous=True)
            with ExitStack() as e2:
                xp1 = self.sb(e2, [128, L + 2], F32, "hxp"); r_xp1 = Res()
                xp = [xp1, xp1]; r_xp = [r_xp1, r_xp1]
                cv = [self.sb(e2, [128, L], F32, "hcv") for _ in range(2)]; r_cv = [Res(), Res()]
                for b in range(2):
                    k.op("dve", lambda e, b=b: e.memset(xp[b][:, 0:1], 0.0), writes=[r_xp[b]])
                    k.op("dve", lambda e, b=b: e.memset(xp[b][:, L + 1:L + 2], 0.0), writes=[r_xp[b]])
                for m in range(2):
                    for b, c6 in enumerate((2 + m, 4 + m)):
                        k.dma("sp", xp[b][:, 1:L + 1], self.projT[P + c6 * 128:P + (c6 + 1) * 128, :], writes=[r_xp[b]])
                        self.dwconv(cv[b], xp[b], 1, 3, lambda jj, c6=c6: hpar[:, c6, jj:jj + 1], hpar[:, c6, 3:4],
                                    r_xp[b], r_hp, r_cv[b])
                    k.op("pool", lambda e, m=m: e.tensor_tensor(out=zT[:, m, :], in0=cv[0], in1=cv[1], op=ALU.mult),
                         reads=[r_cv[0], r_cv[1]], writes=[r_zT[m]])
                    for tt in range(32):
                        pi = tt % 4
                        k.op("pe", lambda e, m=m, tt=tt, pi=pi: e.transpose(
                            out=self.ps[pi][:, 0:128], in_=zT[:, m, tt * 128:(tt + 1) * 128], identity=self.ident),
                            reads=[r_zT[m], self.rg], writes=[self.psr[pi]])
                        if tt % 2 == 0:
                            k.op("act", lambda e, m=m, tt=tt, pi=pi: e.activation(
                                out=KZ[:, tt, 512 + m * 128:512 + (m + 1) * 128], in_=self.ps[pi][:, 0:128], func=AF.Copy),
                                reads=[self.psr[pi]], writes=[r_KZ])
                        else:
                            k.op("dve", lambda e, m=m, tt=tt, pi=pi: e.tensor_copy(
                                out=KZ[:, tt, 512 + m * 128:512 + (m + 1) * 128], in_=self.ps[pi][:, 0:128]),
                                reads=[self.psr[pi]], writes=[r_KZ])
                k.barrier()
            with ExitStack() as e3:
                tabs = [self.sb(e3, [128, 8, 512], BF16, "ftab") for _ in range(2)]; r_tabs = [Res(), Res()]
                ev = self.sb(e3, [128, 6, 2, 512], F32, "hev"); r_ev = Res()
                wt = self.sb(e3, [128, 6, 512], F32, "hwt"); r_wt = Res()
                altcf = self.sb(e3, [128, 1], F32, "altcf"); altcb = self.sb(e3, [128, 1], BF16, "altcb"); r_alt = Res()
                nyq = self.sb(e3, [128, 8], F32, "nyq"); r_nyq = Res()
                k.dma("sp", altcf, self.C["altc"], writes=[r_alt])
                k.op("dve", lambda e: e.tensor_copy(out=altcb, in_=altcf), reads=[r_alt], writes=[r_alt])
                for ctile in range(6):
                    for tt in range(32):
                        k.op("pe", lambda e, tt=tt, ctile=ctile: e.matmul(
                            self.ps[6][:, ctile:ctile + 1], lhsT=KZ[:, tt, ctile * 128:(ctile + 1) * 128], rhs=altcb,
                            start=(tt == 0), stop=(tt == 31)), reads=[r_KZ, r_alt], writes=[self.psr[6]],
                            inc=(tt == 31))
                k.op("dve", lambda e: e.tensor_copy(out=nyq[:, 0:6], in_=self.ps[6][:, 0:6]), reads=[self.psr[6]], writes=[r_nyq])
                nt = 0
                for ch in range(8):
                    fsl = slice(ch * 512, (ch + 1) * 512)
                    for ti, tab in enumerate((self.C["ctab"], self.C["stab"])):
                        for t8 in range(4):
                            b = nt % 2; nt += 1
                            k.dma("sp", tabs[b], tab[t8 * 1024:(t8 + 1) * 1024, fsl].rearrange("(a p) f -> p a f", p=128),
                                  writes=[r_tabs[b]])
                            for a in range(8):
                                tt = t8 * 8 + a
                                for ctile in range(6):
                                    k.op("pe", lambda e, tt=tt, a=a, b=b, ctile=ctile: e.matmul(
                                        self.ps[ctile], lhsT=KZ[:, tt, ctile * 128:(ctile + 1) * 128], rhs=tabs[b][:, a, :],
                                        start=(tt == 0), stop=(tt == 31)), reads=[r_KZ, r_tabs[b]],
                                        writes=[self.psr[ctile]], inc=(tt == 31 or a == 7))
                        for ctile in range(6):
                            eng = "act" if ctile % 2 == 0 else "dve"
                            if eng == "act":
                                k.op("act", lambda e, ctile=ctile, ti=ti: e.activation(out=ev[:, ctile, ti, :], in_=self.ps[ctile],
                                                                                     func=AF.Copy),
                                     reads=[self.psr[ctile]], writes=[r_ev])
                            else:
                                k.op("dve", lambda e, ctile=ctile, ti=ti: e.tensor_copy(out=ev[:, ctile, ti, :], in_=self.ps[ctile]),
                                     reads=[self.psr[ctile]], writes=[r_ev])
                    for m in range(2):
                        KFc, KFs = ev[:, m, 0, :], ev[:, m, 1, :]
                        KBc, KBs = ev[:, 2 + m, 0, :], ev[:, 2 + m, 1, :]
                        Zc, Zs = ev[:, 4 + m, 0, :], ev[:, 4 + m, 1, :]
                        Kr, Ki, t1, t2, Ksum = wt[:, 0, :], wt[:, 1, :], wt[:, 2, :], wt[:, 3, :], wt[:, 4, :]
                        rw = [r_ev, r_wt]
                        k.op("dve", lambda e, Kr=Kr, KFc=KFc, KBc=KBc: e.tensor_tensor(out=Kr, in0=KFc, in1=KBc, op=ALU.add), reads=rw, writes=[r_wt])
                        k.op("pool", lambda e, Ki=Ki, KBs=KBs, KFs=KFs: e.tensor_tensor(out=Ki, in0=KBs, in1=KFs, op=ALU.subtract), reads=rw, writes=[r_wt])
                        k.op("dve", lambda e, t1=t1, Zc=Zc, Kr=Kr: e.tensor_tensor(out=t1, in0=Zc, in1=Kr, op=ALU.mult), reads=rw, writes=[r_wt])
                        k.op("pool", lambda e, t2=t2, Zs=Zs, Ki=Ki: e.tensor_tensor(out=t2, in0=Zs, in1=Ki, op=ALU.mult), reads=rw, writes=[r_wt])
                        k.op("dve", lambda e, t1=t1, t2=t2: e.tensor_tensor(out=t1, in0=t1, in1=t2, op=ALU.add), reads=rw, writes=[r_wt])
                        k.op("act", lambda e, m=m, fsl=fsl, t1=t1: e.activation(out=U[:, m, 0, fsl], in_=t1, func=AF.Identity, scale=2.0 / NFFT),
                             reads=rw, writes=[r_U])
                        k.op("dve", lambda e, t1=t1, Zs=Zs, Kr=Kr: e.tensor_tensor(out=t1, in0=Zs, in1=Kr, op=ALU.mult), reads=rw + [r_U], writes=[r_wt])
                        k.op("pool", lambda e, t2=t2, Zc=Zc, Ki=Ki: e.tensor_tensor(out=t2, in0=Zc, in1=Ki, op=ALU.mult), reads=rw, writes=[r_wt])
                        k.op("dve", lambda e, t1=t1, t2=t2: e.tensor_tensor(out=t1, in0=t1, in1=t2, op=ALU.subtract), reads=rw, writes=[r_wt])
                        k.op("act", lambda e, m=m, fsl=fsl, t1=t1: e.activation(out=U[:, m, 1, fsl], in_=t1, func=AF.Identity, scale=2.0 / NFFT),
                             reads=rw, writes=[r_U])
                        if ch == 0:
                            k.op("dve", lambda e, m=m, Zc=Zc, Kr=Kr: e.scalar_tensor_tensor(
                                out=U[:, m, 0, 0:1], in0=Zc[:, 0:1], scalar=1.0 / NFFT, in1=Kr[:, 0:1], op0=ALU.mult, op1=ALU.mult),
                                reads=rw + [r_U], writes=[r_U])
                            k.op("dve", lambda e, m=m, Ksum=Ksum: e.tensor_tensor(out=Ksum[:, 0:1], in0=nyq[:, m:m + 1], in1=nyq[:, 2 + m:3 + m], op=ALU.add),
                                 reads=rw + [r_nyq], writes=[r_wt])
                            k.op("dve", lambda e, m=m, Ksum=Ksum: e.scalar_tensor_tensor(
                                out=U[:, m, 1, 0:1], in0=nyq[:, 4 + m:5 + m], scalar=1.0 / NFFT, in1=Ksum[:, 0:1], op0=ALU.mult, op1=ALU.mult),
                                reads=rw + [r_U, r_nyq], writes=[r_U])
                k.barrier()
            esK.close()
            with ExitStack() as e4:
                UT = self.sb(e4, [128, 2, 32, 256], BF16, "UT"); r_UT = Res()
                n = 0
                for m in range(2):
                    for ti in range(2):
                        for ft in range(32):
                            pi = n % 4; n += 1
                            k.op("pe", lambda e, m=m, ti=ti, ft=ft, pi=pi: e.transpose(
                                out=self.ps[pi][:, 0:128], in_=U[:, m, ti, ft * 128:(ft + 1) * 128], identity=self.ident),
                                reads=[r_U, self.rg], writes=[self.psr[pi]])
                            dst = UT[:, ti, ft, m * 128:(m + 1) * 128]
                            if n % 2 == 0:
                                k.op("act", lambda e, dst=dst, pi=pi: e.activation(out=dst, in_=self.ps[pi][:, 0:128], func=AF.Copy),
                                     reads=[self.psr[pi]], writes=[r_UT])
                            else:
                                k.op("dve", lambda e, dst=dst, pi=pi: e.tensor_copy(out=dst, in_=self.ps[pi][:, 0:128]),
                                     reads=[self.psr[pi]], writes=[r_UT])
                tabs = [self.sb(e4, [128, 8, 512], BF16, "gtab") for _ in range(2)]; r_tabs = [Res(), Res()]
                xp0 = self.sb(e4, [128, L + 2], F32, "hx0p"); r_xp0 = Res()
                x0c = self.sb(e4, [128, 2, L], F32, "hx0c"); r_x0 = [Res(), Res()]
                k.op("dve", lambda e: e.memset(xp0[:, 0:1], 0.0), writes=[r_xp0])
                k.op("dve", lambda e: e.memset(xp0[:, L + 1:L + 2], 0.0), writes=[r_xp0])
                for m in range(2):
                    k.dma("sp", xp0[:, 1:L + 1], self.projT[P + m * 128:P + (m + 1) * 128, :], writes=[r_xp0])
                    self.dwconv(x0c[:, m, :], xp0, 1, 3, lambda jj, m=m: hpar[:, m, jj:jj + 1], hpar[:, m, 3:4],
                                r_xp0, r_hp, r_x0[m])
                altrf = self.sb(e4, [1, 512], F32, "altrf"); altrb = self.sb(e4, [1, 512], BF16, "altrb"); r_altr = Res()
                k.dma("sp", altrf, self.C["alt"][0:1, :], writes=[r_altr])
                k.op("dve", lambda e: e.tensor_copy(out=altrb, in_=altrf), reads=[r_altr], writes=[r_altr])
                nt = 0
                for ch in range(8):
                    tsl = slice(ch * 512, (ch + 1) * 512)
                    pa = 4 + 2 * (ch % 2)
                    for m in range(2):
                        k.op("pe", lambda e, m=m, pa=pa: e.matmul(
                            self.ps[pa + m], lhsT=UT[0:1, 1, 0, m * 128:(m + 1) * 128], rhs=altrb[0:1, :],
                            start=True, stop=False), reads=[r_UT, r_altr], writes=[self.psr[pa + m]], inc=False)
                    for ti, tab in enumerate((self.C["ctab"], self.C["stab"])):
                        for f8 in range(4):
                            b = nt % 2; nt += 1
                            k.dma("sp", tabs[b], tab[f8 * 1024:(f8 + 1) * 1024, tsl].rearrange("(a p) f -> p a f", p=128),
                                  writes=[r_tabs[b]])
                            for a in range(8):
                                ft = f8 * 8 + a
                                first = False
                                last = (ti == 1 and ft == 31)
                                for m in range(2):
                                    k.op("pe", lambda e, ti=ti, ft=ft, m=m, a=a, b=b, pa=pa, first=first, last=last: e.matmul(
                                        self.ps[pa + m], lhsT=UT[:, ti, ft, m * 128:(m + 1) * 128], rhs=tabs[b][:, a, :],
                                        start=first, stop=last), reads=[r_UT, r_tabs[b]], writes=[self.psr[pa + m]],
                                        inc=(last or a == 7))
                    for m in range(2):
                        k.op("dve", lambda e, m=m, tsl=tsl, pa=pa: e.scalar_tensor_tensor(
                            out=zT[:, m, tsl], in0=zT[:, m, tsl], scalar=hbias[:, m:m + 1], in1=self.ps[pa + m],
                            op0=ALU.mult, op1=ALU.add), reads=[self.psr[pa + m], r_zT[m], r_hp], writes=[r_zT[m]])
                        k.op("pool", lambda e, m=m, tsl=tsl: e.tensor_tensor(out=zT[:, m, tsl], in0=zT[:, m, tsl], in1=x0c[:, m, tsl],
                                                                            op=ALU.mult), reads=[r_zT[m], r_x0[m]], writes=[r_zT[m]])
                for m in range(2):
                    k.dma("sp", self.ymixT[2 * RGW + m * 128:2 * RGW + (m + 1) * 128, :], zT[:, m, :], reads=[r_zT[m]])
                k.barrier()

    def phase3(self, l):
        k = self.k
        hv = self.hA.rearrange("(kt p) t -> p kt t", p=128)
        ho = self.hB.rearrange("(kt p) t -> p kt t", p=128)
        yv = self.ymixT.rearrange("(kt p) t -> p kt t", p=128)
        with ExitStack() as es:
            wo = self.sb(es, [128, 8, D], BF16, "wo"); r_wo = Res()
            k.dma("pool", wo, self.W["w_out"][l].rearrange("(kt p) c -> p kt c", p=128), writes=[r_wo])
            mg, r_mg = self.load_gain(es, self.W["mix_norm_g"][l], "mg")
            yt = [self.sb(es, [128, 8, 512], F32, "yt") for _ in range(2)]; r_yt = [Res(), Res()]
            ht = [self.sb(es, [128, 8, 512], F32, "h3") for _ in range(2)]; r_ht = [Res(), Res()]
            sq = self.sb(es, [128, 8, 512], BF16, "sq3"); r_sq = Res()
            yn = self.sb(es, [128, 8, 512], BF16, "yn"); r_yn = Res()
            rs = self.sb(es, [128, 3, 512], F32, "rs3"); r_rs = Res()
            groups = ((0, 3, float(RGW)), (3, 6, float(S5W)), (6, 8, float(HYW)))
            for tt in range(8):
                b = tt % 2
                tsl = slice(tt * 512, (tt + 1) * 512)
                k.dma("sp", yt[b], yv[:, :, tsl], writes=[r_yt[b]])
                k.dma("sp", ht[b], hv[:, :, tsl], writes=[r_ht[b]])
                k.op("act", lambda e, b=b: e.activation(out=sq, in_=yt[b], func=AF.Square), reads=[r_yt[b]], writes=[r_sq])
                for gi, (a0, a1, wd) in enumerate(groups):
                    self.rms_rstd([sq[:, kt, :] for kt in range(a0, a1)], 512, wd, 5 + gi, rs[:, gi, :], r_sq, r_rs)
                for kt in range(8):
                    gi = 0 if kt < 3 else (1 if kt < 6 else 2)
                    k.op("dve", lambda e, kt=kt, gi=gi, b=b: e.scalar_tensor_tensor(
                        out=yn[:, kt, :], in0=yt[b][:, kt, :], scalar=mg[:, kt:kt + 1], in1=rs[:, gi, :], op0=ALU.mult,
                        op1=ALU.mult), reads=[r_yt[b], r_rs, r_mg], writes=[r_yn])
                for mt in range(8):
                    pi = mt % 4
                    for kt in range(8):
                        k.op("pe", lambda e, mt=mt, kt=kt, pi=pi: e.matmul(
                            self.ps[pi], lhsT=wo[:, kt, mt * 128:(mt + 1) * 128], rhs=yn[:, kt, :], start=(kt == 0),
                            stop=(kt == 7)), reads=[r_wo, r_yn], writes=[self.psr[pi]], inc=(kt == 7))
                    k.op("dve", lambda e, mt=mt, pi=pi, b=b: e.tensor_tensor(out=ht[b][:, mt, :], in0=ht[b][:, mt, :],
                                                                             in1=self.ps[pi], op=ALU.add),
                         reads=[self.psr[pi], r_ht[b]], writes=[r_ht[b]])
                k.dma("sp", ho[:, :, tsl], ht[b], reads=[r_ht[b]])
            k.barrier()

    def phase4(self, l):
        k = self.k
        W = self.W
        hv = self.hB.rearrange("(kt p) t -> p kt t", p=128)
        ho = self.hA.rearrange("(kt p) t -> p kt t", p=128)
        NF = DFF // 128
        with ExitStack() as es:
            wu = self.sb(es, [128, 8, 2 * DFF], BF16, "wu"); r_wu = Res()
            wuv = W["w_up"][l].rearrange("(kt p) c -> p kt c", p=128)
            for kt in range(8):
                for hh in range(2):
                    k.dma("pool", wu[:, kt, hh * DFF:(hh + 1) * DFF], wuv[:, kt, hh * DFF:(hh + 1) * DFF], writes=[r_wu])
            wd = self.sb(es, [128, NF, D], BF16, "wd"); r_wd = Res()
            wdv = W["w_down"][l].rearrange("(ft p) c -> p ft c", p=128)
            for f0 in range(0, NF, 6):
                f1 = min(NF, f0 + 6)
                k.dma("pool", wd[:, f0:f1, :], wdv[:, f0:f1, :], writes=[r_wd])
            g2, r_g2 = self.load_gain(es, W["norm2_g"][l], "g2")
            cw = self.sb(es, [128, 2 * NF, 4], F32, "cw"); r_cw = Res()
            for j in range(3):
                k.dma("sp", cw[:, :, j], W["ffn_conv_w"][l][j].rearrange("(ft p) -> p ft", p=128), writes=[r_cw],
                      allow_slow_non_contiguous=True)
            k.dma("sp", cw[:, :, 3], W["ffn_conv_b"][l].rearrange("(ft p) -> p ft", p=128), writes=[r_cw],
                  allow_slow_non_contiguous=True)
            ht = self.sb(es, [128, 8, 512], F32, "h4"); r_ht = Res()
            sq = self.sb(es, [128, 8, 512], BF16, "sq4"); r_sq = Res()
            nb = self.sb(es, [128, 8, 512], BF16, "nb4"); r_nb = Res()
            rstd = self.sb(es, [128, 512], F32, "rstd4"); r_rstd = Res()
            gT = self.sb(es, [128, NF, 512], BF16, "gT"); r_gT = Res()
            ca = [self.sb(es, [128, 512], F32, "ca") for _ in range(2)]; r_ca = [Res(), Res()]
            cvv = [self.sb(es, [128, 512], F32, "cvv") for _ in range(2)]; r_cv = [Res(), Res()]
            t0 = 0
            it = 0
            while t0 < L:
                nout = min(510, L - t0)
                Wd = nout + 2
                lo = max(t0 - 1, 0); hi = min(t0 + nout + 1, L)
                off = lo - (t0 - 1)
                if off > 0:
                    k.op("dve", lambda e: e.memset(ht[:, :, 0:1], 1.0), writes=[r_ht])
                if hi < t0 + nout + 1:
                    k.op("dve", lambda e, Wd=Wd: e.memset(ht[:, :, Wd - 1:Wd], 1.0), writes=[r_ht])
                k.dma("sp", ht[:, :, off:off + hi - lo], hv[:, :, lo:hi], writes=[r_ht])
                self.norm_tile(ht, Wd, g2, r_g2, sq, nb, rstd, r_ht, r_sq, r_nb, r_rstd, psi=7)
                if off > 0:
                    k.op("dve", lambda e: e.memset(nb[:, :, 0:1], 0.0), reads=[r_nb], writes=[r_nb])
                if hi < t0 + nout + 1:
                    k.op("dve", lambda e, Wd=Wd: e.memset(nb[:, :, Wd - 1:Wd], 0.0), reads=[r_nb], writes=[r_nb])
                for f in range(NF):
                    b = f % 2
                    pa, pv = 2 * b, 2 * b + 1
                    for (pi, ftile) in ((pa, f), (pv, NF + f)):
                        for kt in range(8):
                            k.op("pe", lambda e, pi=pi, ftile=ftile, kt=kt, Wd=Wd: e.matmul(
                                self.ps[pi][:, 0:Wd], lhsT=wu[:, kt, ftile * 128:(ftile + 1) * 128], rhs=nb[:, kt, 0:Wd],
                                start=(kt == 0), stop=(kt == 7)), reads=[r_wu, r_nb], writes=[self.psr[pi]], inc=(kt == 7))
                    for (pi, ftile, dst, r_dst) in ((pa, f, ca[b], r_ca[b]), (pv, NF + f, cvv[b], r_cv[b])):
                        k.op("act", lambda e, pi=pi, ftile=ftile, dst=dst, nout=nout: e.activation(
                            out=dst[:, 0:nout], in_=self.ps[pi][:, 1:1 + nout], func=AF.Identity,
                            scale=cw[:, ftile, 1:2], bias=cw[:, ftile, 3:4]), reads=[self.psr[pi], r_cw], writes=[r_dst])
                        k.op("dve", lambda e, pi=pi, ftile=ftile, dst=dst, nout=nout: e.scalar_tensor_tensor(
                            out=dst[:, 0:nout], in0=self.ps[pi][:, 0:nout], scalar=cw[:, ftile, 0:1], in1=dst[:, 0:nout],
                            op0=ALU.mult, op1=ALU.add), reads=[self.psr[pi], r_cw, r_dst], writes=[r_dst])
                        k.op("dve", lambda e, pi=pi, ftile=ftile, dst=dst, nout=nout: e.scalar_tensor_tensor(
                            out=dst[:, 0:nout], in0=self.ps[pi][:, 2:2 + nout], scalar=cw[:, ftile, 2:3], in1=dst[:, 0:nout],
                            op0=ALU.mult, op1=ALU.add), reads=[self.psr[pi], r_cw, r_dst], writes=[r_dst])
                    k.op("act", lambda e, b=b, nout=nout: e.activation(out=ca[b][:, 0:nout], in_=ca[b][:, 0:nout],
                                                                       func=AF.Gelu_apprx_tanh), reads=[r_ca[b]], writes=[r_ca[b]])
                    k.op("pool", lambda e, b=b, f=f, nout=nout: e.tensor_tensor(out=gT[:, f, 0:nout], in0=ca[b][:, 0:nout],
                                                                               in1=cvv[b][:, 0:nout], op=ALU.mult),
                         reads=[r_ca[b], r_cv[b]], writes=[r_gT])
                for mt in range(8):
                    pi = 4 + mt % 4
                    for f in range(NF):
                        k.op("pe", lambda e, mt=mt, f=f, pi=pi, nout=nout: e.matmul(
                            self.ps[pi][:, 0:nout], lhsT=wd[:, f, mt * 128:(mt + 1) * 128], rhs=gT[:, f, 0:nout],
                            start=(f == 0), stop=(f == NF - 1)), reads=[r_wd, r_gT], writes=[self.psr[pi]],
                            inc=(f == NF - 1))
                    k.op("dve", lambda e, mt=mt, pi=pi, nout=nout: e.tensor_tensor(
                        out=ht[:, mt, 1:1 + nout], in0=ht[:, mt, 1:1 + nout], in1=self.ps[pi][:, 0:nout], op=ALU.add),
                        reads=[self.psr[pi], r_ht], writes=[r_ht])
                k.dma("sp", ho[:, :, t0:t0 + nout], ht[:, :, 1:1 + nout], reads=[r_ht])
                t0 += nout
                it += 1
            k.barrier()

    def phase_final(self):
        k = self.k
        hv = self.hA.rearrange("(kt p) t -> p kt t", p=128)
        with ExitStack() as es:
            fg, r_fg = self.load_gain(es, self.W["final_norm_g"], "fg")
            ht = [self.sb(es, [128, 8, 512], F32, "hf") for _ in range(2)]; r_ht = [Res(), Res()]
            sq = self.sb(es, [128, 8, 512], BF16, "sqf"); r_sq = Res()
            nf = self.sb(es, [128, 8, 512], F32, "nf"); r_nf = Res()
            rstd = self.sb(es, [128, 512], F32, "rstdf"); r_rstd = Res()
            ot = [self.sb(es, [128, D], F32, "ot") for _ in range(2)]; r_ot = [Res(), Res()]
            n = 0
            for tt in range(8):
                b = tt % 2
                k.dma("sp", ht[b], hv[:, :, tt * 512:(tt + 1) * 512], writes=[r_ht[b]])
                self.norm_tile(ht[b], 512, fg, r_fg, sq, nf, rstd, r_ht[b], r_sq, r_nf, r_rstd, psi=7)
                for blk in range(4):
                    ob = n % 2; n += 1
                    for half in range(2):
                        pi = (2 * n + half) % 4
                        for j in range(4):
                            kt = half * 4 + j
                            k.op("pe", lambda e, kt=kt, j=j, pi=pi, blk=blk: e.transpose(
                                out=self.ps[pi][:, j * 128:(j + 1) * 128], in_=nf[:, kt, blk * 128:(blk + 1) * 128],
                                identity=self.ident), reads=[r_nf, self.rg], writes=[self.psr[pi]], inc=(j == 3))
                        if half == 0:
                            k.op("act", lambda e, pi=pi, ob=ob: e.activation(out=ot[ob][:, 0:512], in_=self.ps[pi], func=AF.Copy),
                                 reads=[self.psr[pi]], writes=[r_ot[ob]])
                        else:
                            k.op("dve", lambda e, pi=pi, ob=ob: e.tensor_copy(out=ot[ob][:, 512:1024], in_=self.ps[pi]),
                                 reads=[self.psr[pi]], writes=[r_ot[ob]])
                    row = tt * 512 + blk * 128
                    k.dma("sp", self.out[row:row + 128, :], ot[ob], reads=[r_ot[ob]])
            k.barrier()

    Builder.phase_hy = phase_hy
    Builder.phase3 = phase3
    Builder.phase4 = phase4
    Builder.phase_final = phase_final


_hy_methods()

_NC_CACHE = {}


def build_nc(debug=False, phases=None, depth=DEPTH):
    key = (debug, tuple(phases) if phases else None, depth)
    if key not in _NC_CACHE:
        b = Builder(debug=debug, phases=phases, depth=depth)
        b.build()
        _NC_CACHE[key] = b
    return _NC_CACHE[key]


def make_in_maps(inputs, ncores=NB):
    c = _consts()
    maps = []
    for b in range(ncores):
        m = {"x": np.ascontiguousarray(inputs["x"][b], dtype=np.float32)}
        for n in WEIGHT_NAMES:
            m[n] = np.ascontiguousarray(inputs[n], dtype=np.float32)
        for n in CONST_SHAPES:
            m[n] = c[n]
        maps.append(m)
    return maps


def kernel(**inputs):
    b = build_nc()
    in_maps = make_in_maps(inputs)
    res = run_bass_kernel_spmd(b.nc, in_maps, core_ids=list(range(NB)))
    out = np.stack([np.asarray(r["out"], dtype=np.float32) for r in res.results], axis=0)
    return out
```
